# Optimizing a Trainium2 kernel written in Bass

```python
import jax
import jax.numpy as jnp
from jax import lax
import numpy as np

D_MODEL = 1024
BATCH = 16
SEQ = 2048
DEPTH = 2

CTX_LEN = 256
GRID_W = 64
D_MIX = D_MODEL
HEAD_DIM = 64
D_POOL = D_MIX // 4
D_RWKV = (D_MIX - D_POOL) // 2
D_HGRN = D_MIX - D_POOL - D_RWKV
POOL_WINDOWS = (2, 4, 8, 16)
N_POOL_GROUPS = len(POOL_WINDOWS)
POOL_GROUP = D_POOL // N_POOL_GROUPS
H_RWKV = D_RWKV // HEAD_DIM
H_HGRN = D_HGRN // HEAD_DIM
LORA_W = 64
LORA_A = 64
N_DIR = 2
HGRN_CHUNK = 32
D_RWKV_IN = 3 * D_RWKV + N_DIR * (LORA_W + LORA_A)
D_IN = D_POOL + D_RWKV_IN + 2 * D_HGRN + N_DIR * D_HGRN + D_MIX
ALPHA = float((2 * DEPTH) ** 0.25)
BETA = float((8 * DEPTH) ** -0.25)
LN_EPS = 1e-5
GN_EPS = 64e-5
RMS_EPS = 1e-6

kernel_name = "hybrid_pool_rwkv7_hgrn2_flow_block"


def _layer_norm(x):
    xf = x.astype(jnp.float32)
    mu = jnp.mean(xf, axis=-1, keepdims=True)
    var = jnp.mean(jnp.square(xf - mu), axis=-1, keepdims=True)
    return (xf - mu) * lax.rsqrt(var + LN_EPS)


def _heads(z):
    return z.reshape(z.shape[:-1] + (z.shape[-1] // HEAD_DIM, HEAD_DIM))


def _split_proj(p):
    o0 = D_POOL
    o1 = o0 + D_RWKV_IN
    o2 = o1 + 2 * D_HGRN + N_DIR * D_HGRN
    return p[..., :o0], p[..., o0:o1], p[..., o1:o2], p[..., o2:]


def _centred_mean(u, axis, window):
    n = u.shape[axis]
    left = window // 2
    right = window - 1 - left
    t = jnp.arange(n)
    lo = jnp.maximum(t - left, 0)
    hi = jnp.minimum(t + right, n - 1) + 1
    pad = [(0, 0)] * u.ndim
    pad[axis] = (1, 0)
    cs = jnp.pad(jnp.cumsum(u, axis=axis), pad)
    total = jnp.take(cs, hi, axis=axis) - jnp.take(cs, lo, axis=axis)
    shape = [1] * u.ndim
    shape[axis] = n
    count = (hi - lo).astype(jnp.float32).reshape(shape)
    return total / count


def _pool_branch(u, grid, pool_w, pool_scale):
    b, n, _ = u.shape
    uf = u.astype(jnp.float32)
    if grid:
        rows = n // GRID_W
        uf = uf.reshape(b, rows, GRID_W, D_POOL)
        axis = 2
    else:
        axis = 1
    groups = []
    for g, win in enumerate(POOL_WINDOWS):
        ch = uf[..., g * POOL_GROUP:(g + 1) * POOL_GROUP]
        groups.append(_centred_mean(ch, axis, win) - ch)
    pooled = jnp.stack(groups, axis=-2)
    mixed = jnp.einsum('...gi,gio->...go', pooled, pool_w)
    return mixed.reshape(b, n, D_POOL) * pool_scale


def _shift3(u, w):
    up = jnp.pad(u, ((0, 0), (1, 1), (0, 0)))
    return w[0] * up[:, :-2] + w[1] * up[:, 1:-1] + w[2] * up[:, 2:]


def _rwkv_dir_inputs(u, d, w0, w_up, a0, a_up, k_k, k_a):
    r = u[..., :D_RWKV]
    k = u[..., D_RWKV:2 * D_RWKV]
    v = u[..., 2 * D_RWKV:3 * D_RWKV]
    o = 3 * D_RWKV
    xw = u[..., o + d * LORA_W:o + (d + 1) * LORA_W]
    o2 = o + N_DIR * LORA_W
    xa = u[..., o2 + d * LORA_A:o2 + (d + 1) * LORA_A]
    w_log = -jax.nn.softplus(-(w0[d] + jnp.tanh(xw) @ w_up[d])) - 0.5
    decay = jnp.exp(-jnp.exp(w_log))
    a = jax.nn.sigmoid(a0[d] + xa @ a_up[d])
    kk = _heads(k * k_k)
    kk = kk * lax.rsqrt(jnp.sum(kk * kk, axis=-1, keepdims=True) + 1e-12)
    k_mod = k * (1.0 + (a - 1.0) * k_a)
    return (_heads(r), _heads(decay), _heads(k_mod), _heads(v), kk, _heads(a))


def _rwkv_bonus(ins, r_k_d):
    r, k_mod, v = ins[0], ins[2], ins[3]
    return jnp.sum(r * k_mod * r_k_d, axis=-1, keepdims=True) * v


def _rwkv_scan(state, ins, emit):
    r, w, k, v, kk, a = (jnp.moveaxis(z, 1, 0) for z in ins)
    xs = (w, k, v, kk, a, r) if emit else (w, k, v, kk, a)

    def step(S, inp):
        w_t, k_t, v_t, kk_t, a_t = inp[:5]
        sa = jnp.einsum('bhvk,bhk->bhv', S, kk_t)
        S = (S * w_t[:, :, None, :] - sa[..., None] * (kk_t * a_t)[:, :, None, :]
             + v_t[..., None] * k_t[:, :, None, :])
        y = jnp.einsum('bhvk,bhk->bhv', S, inp[5]) if emit else None
        return S, y

    state, ys = lax.scan(step, state, xs)
    return state, (jnp.moveaxis(ys, 0, 1) if emit else None)


def _rwkv_readout(y, bonus, gn_g, gn_b):
    mu = jnp.mean(y, axis=-1, keepdims=True)
    var = jnp.mean(jnp.square(y - mu), axis=-1, keepdims=True)
    yn = (y - mu) * lax.rsqrt(var + GN_EPS) * gn_g.reshape(H_RWKV, HEAD_DIM) + gn_b.reshape(H_RWKV, HEAD_DIM)
    out = yn + bonus
    return out.reshape(out.shape[:-2] + (D_RWKV,))


def _rwkv_branch(u_ctx, u_lat, w0, w_up, a0, a_up, k_k, k_a, r_k, gn_g, gn_b, emit_ctx):
    b = u_lat.shape[0]
    y_ctx, y_lat, bonus_ctx, bonus_lat = 0.0, 0.0, 0.0, 0.0
    for d in range(N_DIR):
        ins_c = _rwkv_dir_inputs(u_ctx, d, w0, w_up, a0, a_up, k_k, k_a)
        ins_l = _rwkv_dir_inputs(u_lat, d, w0, w_up, a0, a_up, k_k, k_a)
        bonus_lat = bonus_lat + _rwkv_bonus(ins_l, r_k[d])
        if emit_ctx:
            bonus_ctx = bonus_ctx + _rwkv_bonus(ins_c, r_k[d])
        if d == 1:
            ins_c = tuple(jnp.flip(z, axis=1) for z in ins_c)
            ins_l = tuple(jnp.flip(z, axis=1) for z in ins_l)
        s0 = jnp.zeros((b, H_RWKV, HEAD_DIM, HEAD_DIM), jnp.float32)
        s_c, yc = _rwkv_scan(s0, ins_c, emit_ctx)
        _, yl = _rwkv_scan(s_c, ins_l, True)
        if d == 1:
            yl = jnp.flip(yl, axis=1)
            if emit_ctx:
                yc = jnp.flip(yc, axis=1)
        y_lat = y_lat + yl
        if emit_ctx:
            y_ctx = y_ctx + yc
    out_lat = _rwkv_readout(y_lat, bonus_lat, gn_g, gn_b)
    out_ctx = _rwkv_readout(y_ctx, bonus_ctx, gn_g, gn_b) if emit_ctx else None
    return out_ctx, out_lat


def _hgrn_dir_inputs(fh, d, lb_d):
    q = fh[..., :D_HGRN]
    i = fh[..., D_HGRN:2 * D_HGRN]
    z = fh[..., (2 + d) * D_HGRN:(3 + d) * D_HGRN]
    logf = jnp.logaddexp(jnp.log(lb_d), jnp.log1p(-lb_d) + jax.nn.log_sigmoid(z))
    k = (1.0 - lb_d) * jax.nn.sigmoid(-z)
    return (_heads(k), _heads(i), _heads(logf), _heads(q))


def _hgrn_scan(state, ins, emit):
    k, i, logf, q = ins
    b, n, h, dk = k.shape
    nc = n // HGRN_CHUNK

    def chunks(z):
        return z.reshape(b, nc, HGRN_CHUNK, h, dk).transpose(1, 0, 3, 2, 4)

    causal = jnp.tril(jnp.ones((HGRN_CHUNK, HGRN_CHUNK), dtype=bool))[:, :, None]

    def step(S, inp):
        k_c, i_c, lf_c = inp[:3]
        g = jnp.cumsum(lf_c, axis=2)
        g_last = g[:, :, -1:, :]
        if emit:
            q_c = inp[3]
            inter = jnp.einsum('bhtk,bhkv->bhtv', q_c * jnp.exp(g), S)
            rel = jnp.exp(jnp.where(causal, g[:, :, :, None, :] - g[:, :, None, :, :], -jnp.inf))
            scores = jnp.einsum('bhtk,bhsk,bhtsk->bhts', q_c, k_c, rel)
            o = inter + jnp.einsum('bhts,bhsv->bhtv', scores, i_c)
        else:
            o = None
        S = (jnp.exp(g_last[:, :, 0, :])[..., None] * S
             + jnp.einsum('bhsk,bhsv->bhkv', k_c * jnp.exp(g_last - g), i_c))
        return S, o

    xs = (chunks(k), chunks(i), chunks(logf), chunks(q)) if emit else (chunks(k), chunks(i), chunks(logf))
    state, o = lax.scan(step, state, xs)
    if emit:
        o = o.transpose(1, 0, 3, 2, 4).reshape(b, n, h, dk)
    return state, o


def _rms_heads(o, norm_g):
    on = o * lax.rsqrt(jnp.mean(o * o, axis=-1, keepdims=True) + RMS_EPS) * norm_g.reshape(H_HGRN, HEAD_DIM)
    return on.reshape(on.shape[:-2] + (D_HGRN,))


def _hgrn_branch(fh_ctx, fh_lat, lb, norm_g, emit_ctx):
    b = fh_lat.shape[0]
    o_ctx, o_lat = 0.0, 0.0
    for d in range(N_DIR):
        ins_c = _hgrn_dir_inputs(fh_ctx, d, lb[d])
        ins_l = _hgrn_dir_inputs(fh_lat, d, lb[d])
        if d == 1:
            ins_c = tuple(jnp.flip(z, axis=1) for z in ins_c)
            ins_l = tuple(jnp.flip(z, axis=1) for z in ins_l)
        s0 = jnp.zeros((b, H_HGRN, HEAD_DIM, HEAD_DIM), jnp.float32)
        s_c, oc = _hgrn_scan(s0, ins_c, emit_ctx)
        _, ol = _hgrn_scan(s_c, ins_l, True)
        if d == 1:
            ol = jnp.flip(ol, axis=1)
            if emit_ctx:
                oc = jnp.flip(oc, axis=1)
        o_lat = o_lat + ol
        if emit_ctx:
            o_ctx = o_ctx + oc
    out_ctx = _rms_heads(o_ctx, norm_g) if emit_ctx else None
    return out_ctx, _rms_heads(o_lat, norm_g)


def setup_inputs(seed: int = 0) -> dict:
    key = jax.random.key(seed)
    ks = jax.random.split(key, 24)
    f32 = jnp.float32

    def nrm(k, shape, scale):
        return scale * jax.random.normal(k, shape, f32)

    return {
        "x": nrm(ks[0], (BATCH, SEQ, D_MODEL), 1.0),
        "c": nrm(ks[1], (BATCH, D_MODEL), 1.0),
        "ctx": nrm(ks[2], (BATCH, CTX_LEN, D_MODEL), 1.0),
        "c_ctx": nrm(ks[3], (D_MODEL,), 1.0),
        "mod_w": nrm(ks[4], (DEPTH, D_MODEL, 3 * D_MODEL), D_MODEL ** -0.5),
        "mod_b": nrm(ks[5], (DEPTH, 3 * D_MODEL), 0.02),
        "w_in": nrm(ks[6], (DEPTH, D_MODEL, D_IN), D_MODEL ** -0.5),
        "rwkv_shift": jnp.array([0.25, 0.5, 0.25], f32)[None, :, None] + nrm(ks[7], (DEPTH, 3, D_RWKV_IN), 0.05),
        "pool_w": nrm(ks[8], (DEPTH, N_POOL_GROUPS, POOL_GROUP, POOL_GROUP), POOL_GROUP ** -0.5),
        "pool_scale": 1.0 + nrm(ks[9], (DEPTH, D_POOL), 0.1),
        "rwkv_w0": jax.random.uniform(ks[10], (DEPTH, N_DIR, D_RWKV), f32, minval=-5.0, maxval=-0.5),
        "rwkv_w_up": nrm(ks[11], (DEPTH, N_DIR, LORA_W, D_RWKV), 0.5 * LORA_W ** -0.5),
        "rwkv_a0": nrm(ks[12], (DEPTH, N_DIR, D_RWKV), 0.1),
        "rwkv_a_up": nrm(ks[13], (DEPTH, N_DIR, LORA_A, D_RWKV), LORA_A ** -0.5),
        "rwkv_k_k": 0.85 + nrm(ks[14], (DEPTH, D_RWKV), 0.05),
        "rwkv_k_a": 1.0 + nrm(ks[15], (DEPTH, D_RWKV), 0.05),
        "rwkv_r_k": nrm(ks[16], (DEPTH, N_DIR, H_RWKV, HEAD_DIM), 0.1),
        "rwkv_gn_g": 1.0 + nrm(ks[17], (DEPTH, D_RWKV), 0.05),
        "rwkv_gn_b": nrm(ks[18], (DEPTH, D_RWKV), 0.02),
        "hgrn_lb_logits": nrm(ks[19], (N_DIR, DEPTH, D_HGRN), 0.5),
        "hgrn_norm_g": 1.0 + nrm(ks[20], (DEPTH, D_HGRN), 0.05),
        "w_out": nrm(ks[21], (DEPTH, D_MIX, D_MODEL), BETA * D_MIX ** -0.5),
        "ln_g": 1.0 + nrm(ks[22], (DEPTH, D_MODEL), 0.05),
        "ln_b": nrm(ks[23], (DEPTH, D_MODEL), 0.02),
    }


def reference(x, c, ctx, c_ctx, mod_w, mod_b, w_in, rwkv_shift, pool_w, pool_scale,
              rwkv_w0, rwkv_w_up, rwkv_a0, rwkv_a_up, rwkv_k_k, rwkv_k_a, rwkv_r_k,
              rwkv_gn_g, rwkv_gn_b, hgrn_lb_logits, hgrn_norm_g, w_out, ln_g, ln_b):
    out_dtype = x.dtype
    lb_w = jax.nn.softmax(hgrn_lb_logits.astype(jnp.float32), axis=1)
    lb_all = jnp.maximum(jnp.cumsum(lb_w, axis=1) - lb_w[:, :1], 0.0)
    h_x = x.astype(jnp.float32)
    h_ctx = ctx.astype(jnp.float32)
    for l in range(DEPTH):
        last = l == DEPTH - 1
        mod_lat = jax.nn.silu(c.astype(jnp.float32)) @ mod_w[l] + mod_b[l]
        mod_c = jax.nn.silu(c_ctx.astype(jnp.float32)) @ mod_w[l] + mod_b[l]
        sh_l, sc_l, gt_l = jnp.split(mod_lat, 3, axis=-1)
        sh_c, sc_c, gt_c = jnp.split(mod_c, 3, axis=-1)
        p_lat = (_layer_norm(h_x) * (1.0 + sc_l[:, None]) + sh_l[:, None]) @ w_in[l]
        p_ctx = (_layer_norm(h_ctx) * (1.0 + sc_c) + sh_c) @ w_in[l]
        pv_l, u_l, fh_l, g_l = _split_proj(p_lat)
        pv_c, u_c, fh_c, g_c = _split_proj(p_ctx)
        u_l = _shift3(u_l, rwkv_shift[l])
        u_c = _shift3(u_c, rwkv_shift[l])
        y_rc, y_rl = _rwkv_branch(u_c, u_l, rwkv_w0[l], rwkv_w_up[l], rwkv_a0[l], rwkv_a_up[l],
                                  rwkv_k_k[l], rwkv_k_a[l], rwkv_r_k[l], rwkv_gn_g[l], rwkv_gn_b[l],
                                  not last)
        o_hc, o_hl = _hgrn_branch(fh_c, fh_l, lb_all[:, l], hgrn_norm_g[l], not last)
        pool_l = _pool_branch(pv_l, True, pool_w[l], pool_scale[l])
        mix_l = jnp.concatenate([pool_l, y_rl, o_hl], axis=-1) * jax.nn.silu(g_l)
        new_x = _layer_norm(ALPHA * h_x + gt_l[:, None] * (mix_l @ w_out[l])) * ln_g[l] + ln_b[l]
        if not last:
            pool_c = _pool_branch(pv_c, False, pool_w[l], pool_scale[l])
            mix_c = jnp.concatenate([pool_c, y_rc, o_hc], axis=-1) * jax.nn.silu(g_c)
            h_ctx = _layer_norm(ALPHA * h_ctx + gt_c * (mix_c @ w_out[l])) * ln_g[l] + ln_b[l]
        h_x = new_x
    return h_x.astype(out_dtype)
```

```python
import numpy as np
import ml_dtypes
from contextlib import ExitStack
import concourse.bass as bass
import concourse.mybir as mybir
from concourse.bass_utils import run_bass_kernel_spmd

F32 = mybir.dt.float32
BF16 = mybir.dt.bfloat16
AF = mybir.ActivationFunctionType
ALU = mybir.AluOpType
AX = mybir.AxisListType

D = 1024
NB_CORE = 2
CTX = 256
SEQ = 2048
TOK = CTX + SEQ
NTT = TOK // 128
DEPTH = 2
DIN = 4224
ALPHA = float((2 * DEPTH) ** 0.25)
LN_EPS = 1e-5
GN_EPS = 64e-5
RMS_EPS = 1e-6
CDEC = float(np.exp(-0.5))
NB = 256
NPP = 69
XW = 2308
SEM_LIMIT = 30000
N_DMA_SLOTS = 12


class Buf:
    def __init__(self, name, t):
        self.name = name
        self.t = t

    def __getitem__(self, idx):
        return self.t[idx]


class Sched:
    ENGS = ('pe', 'act', 'dve', 'pool', 'sp')

    def __init__(self, nc, stack):
        self.nc = nc
        self.stack = stack
        self.prog = {e: [] for e in self.ENGS}
        self.cnt = {e: 0 for e in self.ENGS}
        self.sem = {}
        self.nsem = 0
        for e in self.ENGS:
            self.sem[e] = self._newsem(e)
        self.known = {e: {} for e in self.ENGS}
        self.snap = {}
        self.track = {}
        self.dma_slots = {q: [[self._newsem('d%s' % q), 0] for _ in range(N_DMA_SLOTS)] for q in ('sp', 'pool', 'act')}
        self.dma_rr = {q: 0 for q in ('sp', 'pool', 'act')}
        self.n_instr = 0
        self.epoch_done = {e: {} for e in self.ENGS}
        self.out_tokens = []

    def _newsem(self, tag):
        self.nsem += 1
        return self.stack.enter_context(self.nc.semaphore('s_%s_%d' % (tag, self.nsem)))

    def sbuf(self, name, shape, dtype):
        return Buf(name, self.stack.enter_context(self.nc.sbuf_tensor(name, list(shape), dtype)))

    def psum(self, name, shape, dtype):
        return Buf(name, self.stack.enter_context(self.nc.psum_tensor(name, list(shape), dtype)))

    def _need(self, eng, tok, waits):
        if tok is None:
            return
        sem, val = tok
        if self.known[eng].get(sem, 0) >= val:
            return
        if waits.get(sem, 0) < val:
            waits[sem] = val

    def _deps(self, eng, reads, writes):
        waits = {}
        for k in reads:
            tr = self.track.get(k)
            if tr is not None:
                self._need(eng, tr[0], waits)
        for k in writes:
            tr = self.track.get(k)
            if tr is not None:
                self._need(eng, tr[0], waits)
                for s, v in tr[1].items():
                    self._need(eng, (s, v), waits)
        out = {}
        for s, v in waits.items():
            if eng == 'pe' and s is self.sem['pe']:
                continue
            out[s] = v
        kn = self.known[eng]
        for s, v in out.items():
            if kn.get(s, 0) < v:
                kn[s] = v
            sn = self.snap.get((s, v))
            if sn:
                for s2, v2 in sn.items():
                    if kn.get(s2, 0) < v2:
                        kn[s2] = v2
        return list(out.items())

    def _record(self, tok, reads, writes):
        s, v = tok
        for k in reads:
            tr = self.track.setdefault(k, [None, {}])
            if tr[1].get(s, 0) < v:
                tr[1][s] = v
        for k in writes:
            self.track[k] = [tok, {}]

    def op(self, eng, fn, reads=(), writes=(), inc=True):
        reads = [r for r in reads if r is not None]
        waits = self._deps(eng, reads, writes)
        if inc and self.cnt[eng] >= SEM_LIMIT:
            self.epoch_done[eng][self.sem[eng]] = self.cnt[eng]
            self.sem[eng] = self._newsem(eng)
            self.cnt[eng] = 0
        sem = self.sem[eng]
        if inc:
            self.cnt[eng] += 1
        tok = (sem, self.cnt[eng] if inc else self.cnt[eng] + 1)
        self.prog[eng].append((fn, waits, (sem, 1) if inc else None))
        if inc:
            sn = dict(self.known[eng])
            sn.update(self.epoch_done[eng])
            self.snap[tok] = sn
        self._record(tok, reads, writes)
        self.n_instr += 1
        return tok

    def dma(self, q, out_ap, in_ap, reads=(), writes=(), **kw):
        reads = [r for r in reads if r is not None]
        slots = self.dma_slots[q]
        i = self.dma_rr[q]
        self.dma_rr[q] = (i + 1) % len(slots)
        slot = slots[i]
        waits = dict(self._deps(q, reads, writes))
        if slot[1] > 0 and self.known[q].get(slot[0], 0) < slot[1]:
            waits[slot[0]] = slot[1]
            self.known[q][slot[0]] = slot[1]
        slot[1] += 16
        tok = (slot[0], slot[1])
        fn = (lambda e, o=out_ap, i_=in_ap, k=kw: e.dma_start(out=o, in_=i_, **k))
        self.prog[q].append((fn, list(waits.items()), (slot[0], 16)))
        self.snap[tok] = dict(self.known[q])
        self._record(tok, reads, writes)
        self.n_instr += 1
        return tok

    def finish(self, eng='sp'):
        waits = {}
        for s, v in self.out_tokens:
            waits[s] = max(waits.get(s, 0), v)
        self.prog[eng].append((None, list(waits.items()), None))

    def emit(self):
        nc = self.nc
        with nc.Block() as block:
            def run(engobj, items):
                for fn, waits, inc in items:
                    for s, v in waits:
                        engobj.wait_ge(s, v)
                    if fn is None:
                        continue
                    ins = fn(engobj)
                    if inc is not None:
                        ins.then_inc(inc[0], inc[1])

            @block.tensor
            def _(e):
                run(e, self.prog['pe'])

            @block.scalar
            def _(e):
                run(e, self.prog['act'])

            @block.vector
            def _(e):
                run(e, self.prog['dve'])

            @block.gpsimd
            def _(e):
                run(e, self.prog['pool'])

            @block.sync
            def _(e):
                run(e, self.prog['sp'])


def _bf(a):
    return np.ascontiguousarray(a).astype(ml_dtypes.bfloat16)


def make_consts():
    p = np.arange(128)
    same = (p[:, None] // 64) == (p[None, :] // 64)
    a = p[:, None] % 64
    b = p[None, :] % 64
    c = {}
    c['idn'] = _bf(np.eye(128, dtype=np.float32))
    ml = [same & (a < b), same & (a <= b), same & (a > b), same & (a >= b)]
    m0 = same & ((a // 2) == (b // 2)) & ((a % 2) == 1) & ((b % 2) == 0)
    dg = (p[:, None] == p[None, :])
    ml += [m0, m0.T]
    lv = []
    for k in range(1, 6):
        n = 2 ** k
        lv.append(same & ((a // (2 * n)) == (b // (2 * n))) & ((a % (2 * n)) >= n) & ((b % (2 * n)) < n))
    ml += lv
    ml += [m.T for m in lv]
    masks = np.stack(ml).astype(np.float32)
    same32 = (p[:, None] // 32) == (p[None, :] // 32)
    ha = (p[:, None] % 64) < 32
    hb = (p[None, :] % 64) >= 32
    mxf = same & ha & hb
    extra = np.stack([same32 & (p[:, None] < p[None, :]), same32 & (p[:, None] > p[None, :]), mxf, mxf.T]).astype(np.float32)
    masks = np.concatenate([masks, extra], axis=0)
    c['masks'] = _bf(masks.transpose(1, 0, 2))
    c['onesbd'] = _bf(same.astype(np.float32))
    e2 = np.zeros((128, 2), np.float32)
    e2[:64, 0] = 1
    e2[64:, 1] = 1
    c['e2'] = _bf(e2)
    m64 = np.ones((128, NB), np.float32)
    m64[:, ::64] = 0
    c['mask64'] = m64
    m32 = np.ones((128, NB), np.float32)
    m32[:, ::32] = 0
    c['mask32'] = m32

    def pool_mat(n, row):
        Ms = []
        for win in (2, 4, 8, 16):
            left = win // 2
            right = win - 1 - left
            M = np.zeros((n, n), np.float64)
            for t in range(n):
                r0 = (t // row) * row
                tt = t - r0
                lo = max(tt - left, 0)
                hi = min(tt + right, row - 1) + 1
                M[t, r0 + lo:r0 + hi] = 1.0 / (hi - lo)
                M[t, t] -= 1.0
            Ms.append(M)
        return Ms
    lat = pool_mat(128, 64)
    c['pool_lat'] = _bf(np.stack([M.T for M in lat], axis=1).astype(np.float32))
    cx = pool_mat(256, 256)
    pc = np.stack([M.T for M in cx], axis=1).astype(np.float32)
    c['pool_ctx'] = _bf(pc.reshape(2, 128, 4, 256).transpose(1, 0, 2, 3))
    return c


CONST_SHAPES = {'idn': ([128, 128], BF16), 'masks': ([128, 20, 128], BF16), 'onesbd': ([128, 128], BF16),
                'e2': ([128, 2], BF16), 'mask64': ([128, NB], F32), 'mask32': ([128, NB], F32), 'pool_lat': ([128, 4, 128], BF16),
                'pool_ctx': ([128, 2, 4, 256], BF16)}


class _Stop(Exception):
    pass


def build_program(debug=(), depth=DEPTH, nbatch=NB_CORE, stop=None):
    nc = bass.Bass("TRN2", target_bir_lowering=False)
    dram = {}

    def din(name, shape, dt=F32):
        dram[name] = nc.dram_tensor(name, list(shape), dt, kind="ExternalInput").ap()
        return dram[name]

    hin = din("hin", [NB_CORE, TOK, D])
    cT_d = din("cT", [128, 8, 3])
    mod_w = din("mod_w", [DEPTH, D, 3 * D])
    mod_b = din("mod_b", [DEPTH, 3 * D])
    w_in = din("w_in", [DEPTH, D, DIN])
    w_out = din("w_out", [DEPTH, D, D])
    w_up = din("w_up", [DEPTH, 128, 384])
    a_up = din("a_up", [DEPTH, 128, 384])
    pw_d = din("pw", [DEPTH, 128, 2, 64])
    pp_d = din("pp", [DEPTH, 128, NPP])
    gn_g = din("gn_g", [DEPTH, 384])
    gn_b = din("gn_b", [DEPTH, 384])
    hg_g = din("hg_g", [DEPTH, 384])
    pscale = din("pscale", [DEPTH, 256])
    ln_g = din("ln_g", [DEPTH, D])
    ln_b = din("ln_b", [DEPTH, D])
    cd = {k: din("c_" + k, sh, dt) for k, (sh, dt) in CONST_SHAPES.items()}
    out_d = nc.dram_tensor("out", [NB_CORE, SEQ, D], F32, kind="ExternalOutput").ap()
    H1 = nc.dram_tensor("h1_scr", [NB_CORE, TOK, D], F32, kind="Internal").ap()
    MODS = nc.dram_tensor("mods_scr", [3, 3 * D], F32, kind="Internal").ap()
    YB = nc.dram_tensor("yb_scr", [NTT, 128, 384], BF16, kind="Internal").ap()
    OB = nc.dram_tensor("ob_scr", [NTT, 128, 384], BF16, kind="Internal").ap()
    XND = nc.dram_tensor("xn_scr", [128, 8, XW], BF16, kind="Internal").ap()
    dbg_outs = {}

    with ExitStack() as st:
        S = Sched(nc, st)
        sb = S.sbuf

        def dbg(name, ap, shape, dt, reads):
            if name not in debug or name in dbg_outs:
                return
            t = nc.dram_tensor("dbg_" + name, list(shape), dt, kind="ExternalOutput").ap()
            dbg_outs[name] = t
            S.out_tokens.append(S.dma('sp', t, ap, reads=reads))

        WIN = sb("WIN", [128, 8, DIN], BF16)
        WOUT = sb("WOUT", [128, 8, D], BF16)
        XNT = [sb("XNT0", [128, 8, 128], BF16)] * 2
        XBS = [sb("XB%d" % i, [128, 8, NB + 2], BF16) for i in range(2)]
        ZB = sb("ZB", [128, 8, 1], BF16)
        HT = sb("HT", [128, 1056], F32)
        T1 = sb("T1", [128, 1056], F32)
        LNG = sb("LNG", [128, D], F32)
        LNB = sb("LNB", [128, D], F32)
        GT = sb("GT", [128, D], F32)
        GNG = sb("GNG", [128, 384], F32)
        GNB = sb("GNB", [128, 384], F32)
        HGG = sb("HGG", [128, 384], F32)
        PSC = sb("PSC", [128, 256], F32)
        PP = sb("PP", [128, NPP], F32)
        PD = sb("PD", [128, 16], F32)
        WUP = sb("WUP", [128, 384], BF16)
        AUP = sb("AUP", [128, 384], BF16)
        PW = sb("PW", [128, 2, 64], BF16)
        SCSH = sb("SCSH", [128, 3, 2, 8], F32)
        CTs = sb("CTs", [128, 8, 3], F32)
        MODSB = sb("MODSB", [3, 128], F32)
        MODB = sb("MODB", [3, 128], F32)
        S1 = sb("S1", [128, NTT, 6], F32)
        SH1 = sb("SH1", [128, NTT, 6], F32)
        SH0 = sb("SH0", [128, NB // 128, 6], F32)
        C = {k: sb("C_" + k, sh, dt) for k, (sh, dt) in CONST_SHAPES.items()}
        IDN, MASKS, ONESBD, E2, MASK64 = C['idn'], C['masks'], C['onesbd'], C['e2'], C['mask64']
        NT = NB // 128
        R_t = [sb("R_t%d" % i, [128, NB], F32) for i in range(3)]
        K_t = [sb("K_t%d" % i, [128, NB], F32) for i in range(3)]
        V_b = [sb("V_b%d" % i, [128, NB], BF16) for i in range(3)]
        TANHA = sb("TANHA", [128, NB], BF16)
        XAB = sb("XAB", [128, NB], BF16)
        SCR = [sb("SCR%d" % i, [128, NB], F32) for i in range(8)]
        SQB = sb("SQB", [128, NB], BF16)
        RKB = sb("RKB", [128, NB], BF16)
        RT = [sb("RT%d" % i, [128, NB], BF16) for i in range(3)]
        KP = [sb("KP%d" % i, [128, NB], BF16) for i in range(3)]
        KT = [sb("KT%d" % i, [128, NB], BF16) for i in range(3)]
        BN = [sb("BN%d" % i, [128, NB], BF16) for i in range(3)]
        QT = [sb("QT%d" % i, [128, NB], BF16) for i in range(3)]
        KH = [sb("KH%d" % i, [128, NB], BF16) for i in range(3)]
        QX = [XAB] + [sb("QX%d" % i, [128, NB], BF16) for i in range(1, 3)]
        QI = [sb("QI%d" % i, [128, NB], BF16) for i in range(3)]
        KS = [SQB, RKB, TANHA]
        FAC = sb("FAC", [128, 2, NB // 32], F32)
        GAM = sb("GAM", [128, 3, NB // 64], F32)
        GAMH = sb("GAMH", [128, 3, NB // 64], F32)
        V_tm = sb("V_tm", [128, NT, 384], BF16)
        K_tm = sb("K_tm", [128, NT, 384], BF16)
        B_tm = sb("B_tm", [128, NT, 384], BF16)
        KH_tm = sb("KH_tm", [128, NT, 384], BF16)
        I_tm = sb("I_tm", [128, NT, 384], BF16)
        S0 = sb("S0", [128, NT, 6], F32)
        AT_sb = sb("AT_sb", [128, 6, 128], BF16)
        PT_sb = sb("PT_sb", [128, 6, 128], BF16)
        QN_sb = sb("QN_sb", [128, 6, 128], BF16)
        PH_sb = sb("PH_sb", [128, 6, 128], BF16)
        XA = sb("XA", [128, 6, 128], BF16)
        XTb = sb("XTb", [128, 6, 128], BF16)
        TTb = sb("TTb", [128, 6, 128], BF16)
        TNb = sb("TNb", [128, 6, 128], BF16)
        M1b = sb("M1b", [128, 6, 128], BF16)
        M1p = sb("M1p", [128, 6, 128], BF16)
        R_sb = sb("R_sb", [128, 384], BF16)
        U_sb = sb("U_sb", [128, 384], BF16)
        HR32 = sb("HR32", [128, 384], F32)
        HR16 = sb("HR16", [128, 384], BF16)
        HH32 = sb("HH32", [128, 384], F32)
        HH16 = sb("HH16", [128, 384], BF16)
        YST = sb("YST", [128, NT, 384], BF16)
        OST = sb("OST", [128, NT, 384], BF16)
        YBT, OBT, Y_sb, O_sb = YST, OST, YST, OST
        MIX = sb("MIX", [128, D], BF16)
        XH = MIX
        SG = sb("SG", [128, D], BF16)
        MIXT = sb("MIXT", [128, 8, 128], BF16)
        PV_tm = sb("PV_tm", [128, NT, 256], BF16)
        PLT = sb("PLT", [128, 2, 256], BF16)
        W384 = [sb("W384_%d" % i, [128, 384], F32) for i in range(2)]
        ST6 = sb("ST6", [128, 2, 6], F32)
        SM = sb("SM", [128, 64], F32)
        PS = S.psum("PS", [128, 3584], F32)
        PST = S.psum("PST", [128, 1024], BF16)

        def bank(i):
            return ('ps', i)

        PSTK = [('ps', 7)]

        def psl(b, c0, n):
            return PS[:, b * 512 + c0: b * 512 + c0 + n]

        def act(out, in_, func, R, W, bias=None, scale=None):
            kw = {}
            if bias is not None:
                kw['bias'] = bias
            if scale is not None:
                kw['scale'] = scale
            S.op('act', lambda e: e.activation(out=out, in_=in_, func=func, **kw), reads=R, writes=W)

        def tt(eng, out, in0, in1, op, R, W):
            S.op(eng, lambda e: e.tensor_tensor(out=out, in0=in0, in1=in1, op=op), reads=R, writes=W)

        def ts(eng, out, in0, s1, s2, op0, op1, R, W):
            if s2 is None:
                S.op(eng, lambda e: e.tensor_scalar(out=out, in0=in0, scalar1=s1, scalar2=None, op0=op0), reads=R, writes=W)
            else:
                S.op(eng, lambda e: e.tensor_scalar(out=out, in0=in0, scalar1=s1, scalar2=s2, op0=op0, op1=op1), reads=R, writes=W)

        def stt(out, in0, scalar, in1, op0, op1, R, W):
            S.op('dve', lambda e: e.scalar_tensor_tensor(out=out, in0=in0, scalar=scalar, in1=in1, op0=op0, op1=op1), reads=R, writes=W)

        def cp(eng, out, in_, R, W):
            if eng == 'act':
                S.op('act', lambda e: e.copy(out=out, in_=in_), reads=R, writes=W)
            else:
                S.op(eng, lambda e: e.tensor_copy(out=out, in_=in_), reads=R, writes=W)

        def mm(out, lhsT, rhs, start, stop, R, W, inc=True):
            S.op('pe', lambda e: e.matmul(out, lhsT=lhsT, rhs=rhs, start=start, stop=stop), reads=R, writes=W, inc=inc)

        def tr(out, in_, R, W, inc=True):
            S.op('pe', lambda e: e.transpose(out=out, in_=in_, identity=IDN[:, :]), reads=list(R) + [IDN], writes=W, inc=inc)

        dq = ['sp', 'pool']
        dqi = [0]

        def dma(out, in_, R, W, q=None, **kw):
            if q is None:
                q = dq[dqi[0] % 2]
                dqi[0] += 1
            return S.dma(q, out, in_, reads=R, writes=W, **kw)

        def stop_at(name):
            if stop == name:
                raise _Stop()

        try:
            for k in C:
                dma(C[k][tuple(slice(None) for _ in CONST_SHAPES[k][0])], cd[k], [], [C[k]])
            S.op('dve', lambda e: e.memset(ZB[:, :, :], 0.0), writes=[ZB])
            dma(CTs[:, :, :], cT_d, [], [CTs])
            S.op('pool', lambda e: e.memset(R_sb[:, :], 0.0), writes=[R_sb])
            S.op('pool', lambda e: e.memset(U_sb[:, :], 0.0), writes=[U_sb])
            act(CTs[:, :, :], CTs[:, :, :], AF.Silu, [CTs], [CTs])

            for l in range(depth):
                last = (l == DEPTH - 1)
                stg = [HT, T1]
                si = 0
                for k in range(8):
                    for c0 in range(0, DIN, 1056):
                        sgt = stg[si % 2]
                        si += 1
                        dma(sgt[:, 0:1056], w_in[l, k * 128:(k + 1) * 128, c0:c0 + 1056], [], [sgt])
                        cp('pool', WIN[:, k, c0:c0 + 1056], sgt[:, 0:1056], [sgt], [WIN])
                for k in range(8):
                    sgt = stg[si % 2]
                    si += 1
                    dma(sgt[:, 0:D], w_out[l, k * 128:(k + 1) * 128, :], [], [sgt])
                    cp('pool', WOUT[:, k, :], sgt[:, 0:D], [sgt], [WOUT])
                dma(W384[0][:, :], w_up[l], [], [W384[0]])
                cp('pool', WUP[:, :], W384[0][:, :], [W384[0]], [WUP])
                dma(W384[1][:, :], a_up[l], [], [W384[1]])
                cp('pool', AUP[:, :], W384[1][:, :], [W384[1]], [AUP])
                dma(W384[0][:, 0:128].rearrange("p (a o) -> p a o", a=2), pw_d[l], [], [W384[0]])
                cp('pool', PW[:, :, :], W384[0][:, 0:128].rearrange("p (a o) -> p a o", a=2), [W384[0]], [PW])
                dma(PP[:, :], pp_d[l], [], [PP])
                dma(LNG[:, :], ln_g[l:l + 1, :].partition_broadcast(128), [], [LNG])
                dma(LNB[:, :], ln_b[l:l + 1, :].partition_broadcast(128), [], [LNB])
                dma(GNG[:, :], gn_g[l:l + 1, :].partition_broadcast(128), [], [GNG])
                dma(GNB[:, :], gn_b[l:l + 1, :].partition_broadcast(128), [], [GNB])
                dma(HGG[:, :], hg_g[l:l + 1, :].partition_broadcast(128), [], [HGG])
                dma(PSC[:, :], pscale[l:l + 1, :].partition_broadcast(128), [], [PSC])
                ts('dve', PD[:, 0:3], PP[:, 48:51], -1.0, 1.0, ALU.mult, ALU.add, [PP], [PD])
                if l == 0:
                    S.op('pool', lambda e: e.memset(PD[:, 3:9], 0.0), writes=[PD])
                    S.op('pool', lambda e: e.memset(PD[:, 9:15], 1.0), reads=[PD], writes=[PD])
                else:
                    for d in range(2):
                        tt('dve', PD[:, 3 + 3 * d:6 + 3 * d], PP[:, 60 + d * 6:63 + d * 6], PP[:, 57 + d * 6:60 + d * 6], ALU.subtract, [PP, PD], [PD])
                    act(PD[:, 3:9], PD[:, 3:9], AF.Sigmoid, [PD], [PD])
                    ts('dve', PD[:, 9:15], PD[:, 3:9], -1.0, 1.0, ALU.mult, ALU.add, [PD], [PD])
                stop_at('weights')
                for cg in range(24):
                    sgt = stg[si % 2]
                    si += 1
                    dma(sgt[:, 0:1024].rearrange("p (k n) -> p k n", k=8),
                        mod_w[l, :, cg * 128:(cg + 1) * 128].rearrange("(k p) n -> p k n", p=128), [], [sgt])
                    dma(MODB[:, :], mod_b[l:l + 1, cg * 128:(cg + 1) * 128].partition_broadcast(3), [], [MODB])
                    for k in range(8):
                        mm(PS[0:3, 1024:1024 + 128], CTs[:, k, :], sgt[:, k * 128:(k + 1) * 128], k == 0, k == 7, [CTs, sgt], [bank(2)], inc=(k == 7))
                    tt('dve', MODSB[:, :], PS[0:3, 1024:1024 + 128], MODB[:, :], ALU.add, [bank(2), MODB], [MODSB])
                    dma(MODS[:, cg * 128:(cg + 1) * 128], MODSB[:, :], [MODSB], ['MODS'], q='sp')
                for who in range(3):
                    dma(SCSH[:, who, :, :], MODS[who, 0:2048].rearrange("(j k p) -> p j k", j=2, k=8, p=128), ['MODS'], [SCSH],
                        q='sp', allow_slow_non_contiguous=True)
                ts('dve', SCSH[:, :, 1, :], SCSH[:, :, 1, :], 1.0, None, ALU.add, None, [SCSH], [SCSH])
                dbg('scsh', SCSH[:, :, :, :], [128, 3, 2, 8], F32, [SCSH])

                stop_at('mod')
                for b in range(nbatch):
                    h_in = hin[b] if l == 0 else H1[b]
                    for j in range(NTT):
                        who = 2 if j < 2 else b
                        base = 1 + j * 128 if j < 2 else 259 + (j - 2) * 128
                        dma(HT[:, 0:D], h_in[j * 128:(j + 1) * 128, :], ['H1'] if l > 0 else [], [HT])
                        for hh in range(2):
                            S.op('dve', lambda e, hh=hh: e.bn_stats(out=ST6[:, hh, :], in_=HT[:, hh * 512:(hh + 1) * 512]), reads=[HT, ST6], writes=[ST6])
                        S.op('dve', lambda e: e.bn_aggr(out=SM[:, 0:2], in_=ST6[:, :, :]), reads=[ST6], writes=[SM])
                        act(SM[:, 2:3], SM[:, 1:2], AF.Sqrt, [SM], [SM], bias=LN_EPS)
                        S.op('dve', lambda e: e.reciprocal(out=SM[:, 3:4], in_=SM[:, 2:3]), reads=[SM], writes=[SM])
                        ts('dve', SM[:, 4:5], SM[:, 0:1], SM[:, 3:4], -1.0, ALU.mult, ALU.mult, [SM], [SM])
                        act(XH[:, :], HT[:, 0:D], AF.Identity, [HT, SM], [XH], bias=SM[:, 4:5], scale=SM[:, 3:4])
                        for k in range(8):
                            tr(PST[:, k * 128:(k + 1) * 128], XH[:, k * 128:(k + 1) * 128], [XH], PSTK, inc=(k == 7))
                        xnt = XNT[j % 2]
                        for k in range(8):
                            if k % 2 == 0:
                                ts('dve', xnt[:, k, :], PST[:, k * 128:(k + 1) * 128], SCSH[:, who, 1, k:k + 1], SCSH[:, who, 0, k:k + 1],
                                   ALU.mult, ALU.add, PSTK + [SCSH], [xnt])
                            else:
                                act(xnt[:, k, :], PST[:, k * 128:(k + 1) * 128], AF.Identity, PSTK + [SCSH], [xnt],
                                    bias=SCSH[:, who, 0, k:k + 1], scale=SCSH[:, who, 1, k:k + 1])
                        dma(XND[:, :, base:base + 128], xnt[:, :, :], [xnt], ['XND'])

                    stop_at('xn')
                    blk_i = [0]
                    for rev in (True, False):
                        d = 1 if rev else 0
                        for hb in (HR32, HH32, HR16, HH16):
                            S.op('pool', lambda e, hb=hb: e.memset(hb[:, :], 0.0), reads=[hb], writes=[hb])
                        blocks = [(True, 0)] + [(False, t0) for t0 in (range(SEQ - NB, -1, -NB) if rev else range(0, SEQ, NB))]
                        for (is_ctx, t0) in blocks:
                            emit = not (is_ctx and last)
                            base = 1 if is_ctx else 259 + t0
                            jt0 = 0 if is_ctx else 2 + t0 // 128
                            first_dbg = (l == 0 and b == 0 and is_ctx)
                            tag = 'r' if rev else 'f'
                            if (not rev) and emit:
                                for t_ in range(NB // 128):
                                    dma(YBT[:, t_, :], YB[jt0 + t_], [('YB', jt0 + t_)], [YBT])
                                    dma(OBT[:, t_, :], OB[jt0 + t_], [('OB', jt0 + t_)], [OBT])
                                if is_ctx or t0 == 0:
                                    who_g = 2 if is_ctx else b
                                    dma(GT[:, :], MODS[who_g:who_g + 1, 2048:3072].partition_broadcast(128), ['MODS'], [GT], q='sp')

                            XN = XBS[blk_i[0] % 2]
                            blk_i[0] += 1
                            lz = is_ctx or t0 == 0
                            rz = is_ctx or t0 == SEQ - NB
                            dma(XN[:, :, (1 if lz else 0):(NB + 1 if rz else NB + 2)], XND[:, :, base - (0 if lz else 1): base + NB + (0 if rz else 1)], ['XND'], [XN])
                            if lz:
                                cp('dve', XN[:, :, 0:1], ZB[:, :, :], [ZB, XN], [XN])
                            if rz:
                                cp('dve', XN[:, :, NB + 1:NB + 2], ZB[:, :, :], [ZB, XN], [XN])
                            if first_dbg and rev:
                                dbg('xn', XN[:, :, :], [128, 8, NB + 2], BF16, [XN])

                            def xn_h(k):
                                return XN[:, k, 0: NB + 2]

                            def xn_c(k):
                                return XN[:, k, 1: NB + 1]

                            pb = [0]

                            def proj_fm(col0, halo):
                                bk = pb[0] % 2
                                pb[0] += 1
                                n = NB + 2 if halo else NB
                                for k in range(8):
                                    mm(psl(bk, 0, n), WIN[:, k, col0:col0 + 128], xn_h(k) if halo else xn_c(k), k == 0, k == 7, [WIN, XN], [bank(bk)], inc=(k == 7))
                                return bk

                            def shift3(c, dst, dst_buf, func=None):
                                bk = proj_fm(256 + c * 128, True)
                                t1, t2 = SCR[6], SCR[7]
                                act(t1[:, :], psl(bk, 1, NB), AF.Identity, [bank(bk), PP], [t1], scale=PP[:, 3 * c + 1:3 * c + 2])
                                stt(t2[:, :], psl(bk, 0, NB), PP[:, 3 * c:3 * c + 1], t1[:, :], ALU.mult, ALU.add, [bank(bk), PP, t1], [t2])
                                if func is None:
                                    stt(dst, psl(bk, 2, NB), PP[:, 3 * c + 2:3 * c + 3], t2[:, :], ALU.mult, ALU.add, [bank(bk), PP, t2], [dst_buf])
                                else:
                                    stt(t1[:, :], psl(bk, 2, NB), PP[:, 3 * c + 2:3 * c + 3], t2[:, :], ALU.mult, ALU.add, [bank(bk), PP, t2], [t1])
                                    act(dst, t1[:, :], func, [t1], [dst_buf])

                            for i in range(3):
                                shift3(i, R_t[i][:, :], R_t[i])
                                shift3(3 + i, K_t[i][:, :], K_t[i])
                                shift3(6 + i, V_b[i][:, :], V_b[i])
                            shift3(9, TANHA[:, :], TANHA, func=AF.Tanh)
                            shift3(10, XAB[:, :], XAB)
                            if first_dbg:
                                dbg('r0' + tag, R_t[0][:, :], [128, NB], F32, [R_t[0]])
                                dbg('v0' + tag, V_b[0][:, :], [128, NB], BF16, [V_b[0]])

                            stop_at('shift')
                            hp_d = slice(d * 64, d * 64 + 64)
                            for i in range(3):
                                sig, a_, kk, tmp, gi, ge = SCR[0], SCR[1], SCR[2], SCR[3], SCR[4], SCR[5]
                                e1, e2_, e3 = SCR[6], SCR[7], SCR[3]
                                mm(psl(2, 0, NB), WUP[hp_d, i * 128:(i + 1) * 128], TANHA[hp_d, :], True, True, [WUP, TANHA], [bank(2)])
                                act(sig[:, :], psl(2, 0, NB), AF.Sigmoid, [bank(2), PP], [sig], bias=PP[:, 33 + d * 3 + i:34 + d * 3 + i])
                                mm(psl(2, 0, NB), AUP[hp_d, i * 128:(i + 1) * 128], XAB[hp_d, :], True, True, [AUP, XAB], [bank(2)])
                                act(a_[:, :], psl(2, 0, NB), AF.Sigmoid, [bank(2), PP], [a_], bias=PP[:, 39 + d * 3 + i:40 + d * 3 + i])
                                act(SQB[:, :], K_t[i][:, :], AF.Square, [K_t[i], PP], [SQB], scale=PP[:, 45 + i:46 + i])
                                mm(psl(2, 0, NB), ONESBD[:, :], SQB[:, :], True, True, [ONESBD, SQB], [bank(2)])
                                act(tmp[:, :], psl(2, 0, NB), AF.Sqrt, [bank(2)], [tmp], bias=1e-12)
                                S.op('dve', lambda e, tmp=tmp: e.reciprocal(out=tmp[:, :], in_=tmp[:, :]), reads=[tmp], writes=[tmp])
                                stt(kk[:, :], K_t[i][:, :], PP[:, 45 + i:46 + i], tmp[:, :], ALU.mult, ALU.mult, [K_t[i], PP, tmp], [kk])
                                ts('dve', tmp[:, :], a_[:, :], PP[:, 48 + i:49 + i], PD[:, i:i + 1], ALU.mult, ALU.add, [a_, PP, PD], [tmp])
                                tt('pool', tmp[:, :], tmp[:, :], K_t[i][:, :], ALU.mult, [tmp, K_t[i]], [tmp])
                                stt(RKB[:, :], R_t[i][:, :], PP[:, 51 + d * 3 + i:52 + d * 3 + i], tmp[:, :], ALU.mult, ALU.mult, [R_t[i], PP, tmp], [RKB])
                                for t_ in range(NT):
                                    mm(PS[:, 3072 + 300 + t_ * 8 + 2 * i: 3072 + 300 + t_ * 8 + 2 * i + 2], RKB[:, t_ * 128:(t_ + 1) * 128], E2[:, :], True, True,
                                       [RKB, E2], [bank(6)], inc=(t_ == NT - 1))
                                stt(a_[:, :], a_[:, :], -1.0, kk[:, :], ALU.mult, ALU.mult, [a_, kk], [a_])
                                S.op('dve', lambda e, gi=gi, sig=sig: e.tensor_tensor_scan(out=gi[:, :], data0=MASK64[:, :], data1=sig[:, :], initial=0.0, op0=ALU.mult, op1=ALU.add),
                                     reads=[MASK64, sig], writes=[gi])
                                tt('pool', ge[:, :], gi[:, :], sig[:, :], ALU.subtract, [gi, sig], [ge])
                                act(GAM[:, i, :], gi[:, 63::64], AF.Exp, [gi], [GAM], scale=-CDEC)
                                if not rev:
                                    act(e1[:, :], gi[:, :], AF.Exp, [gi], [e1], scale=-CDEC)
                                    act(e2_[:, :], ge[:, :], AF.Exp, [ge], [e2_], scale=-CDEC)
                                    act(sig[:, :], gi[:, :], AF.Exp, [gi], [sig], scale=CDEC)
                                else:
                                    act(e1[:, :], ge[:, :], AF.Exp, [ge], [e1], scale=CDEC)
                                    act(e2_[:, :], gi[:, :], AF.Exp, [gi], [e2_], scale=CDEC)
                                    act(sig[:, :], ge[:, :], AF.Exp, [ge], [sig], scale=-CDEC)
                                tt('dve', RT[i][:, :], R_t[i][:, :], e1[:, :], ALU.mult, [R_t[i], e1], [RT[i]])
                                tt('pool', KP[i][:, :], kk[:, :], e2_[:, :], ALU.mult, [kk, e2_], [KP[i]])
                                tt('dve', KT[i][:, :], tmp[:, :], sig[:, :], ALU.mult, [tmp, sig], [KT[i]])
                                tt('pool', BN[i][:, :], a_[:, :], sig[:, :], ALU.mult, [a_, sig], [BN[i]])
                            sdst = S1 if rev else S0
                            if rev:
                                cp('act', S1[:, jt0:jt0 + NT, :], PS[:, 3072 + 300:3072 + 300 + NT * 8].rearrange("p (t c) -> p t c", c=8)[:, :, 0:6], [bank(6)], [S1])
                            else:
                                cp('act', S0[:, :, :], PS[:, 3072 + 300:3072 + 300 + NT * 8].rearrange("p (t c) -> p t c", c=8)[:, :, 0:6], [bank(6)], [S0])
                            if first_dbg:
                                dbg('rt0' + tag, RT[0][:, :], [128, NB], BF16, [RT[0]])
                                dbg('kp0' + tag, KP[0][:, :], [128, NB], BF16, [KP[0]])
                                dbg('kt0' + tag, KT[0][:, :], [128, NB], BF16, [KT[0]])
                                dbg('bn0' + tag, BN[0][:, :], [128, NB], BF16, [BN[0]])
                                dbg('gam' + tag, GAM[:, :, :], [128, 3, NB // 64], F32, [GAM])

                            stop_at('rprep')
                            NS = NB // 32
                            NC4 = NB // 64
                            v32 = lambda t_: t_[:, :].rearrange("p (c j) -> p c j", c=NS)
                            for i in range(3):
                                sg_, f_, lf, gi, ge, e1, e2_ = SCR[0], SCR[1], SCR[2], SCR[4], SCR[5], SCR[6], SCR[7]
                                bz = proj_fm(1664 + (2 + d) * 384 + i * 128, False)
                                act(sg_[:, :], psl(bz, 0, NB), AF.Sigmoid, [bank(bz)], [sg_])
                                ts('dve', f_[:, :], sg_[:, :], PD[:, 9 + 3 * d + i:10 + 3 * d + i], PD[:, 3 + 3 * d + i:4 + 3 * d + i], ALU.mult, ALU.add, [sg_, PD], [f_])
                                act(lf[:, :], f_[:, :], AF.Ln, [f_], [lf])
                                ts('pool', f_[:, :], f_[:, :], -1.0, 1.0, ALU.mult, ALU.add, [f_], [f_])
                                S.op('dve', lambda e, gi=gi, lf=lf: e.tensor_tensor_scan(out=gi[:, :], data0=C['mask32'][:, :], data1=lf[:, :], initial=0.0, op0=ALU.mult, op1=ALU.add),
                                     reads=[C['mask32'], lf], writes=[gi])
                                if not rev:
                                    gsrc = gi
                                else:
                                    tt('pool', ge[:, :], gi[:, :], lf[:, :], ALU.subtract, [gi, lf], [ge])
                                    gsrc = ge
                                tt('dve', v32(sg_), gi[:, 31::32].unsqueeze(2).to_broadcast([128, NS, 32]), v32(gsrc), ALU.subtract, [gi, gsrc], [sg_])
                                ts('dve', lf[:, :], gsrc[:, :], -80.0, None, ALU.max, None, [gsrc], [lf])
                                act(SM[:, 40:40 + NS], gi[:, 31::32], AF.Exp, [gi, SM], [SM])
                                tt('dve', GAMH[:, i, :], SM[:, 40:40 + NS:2], SM[:, 41:40 + NS:2], ALU.mult, [SM], [GAMH])
                                S.op('pool', lambda e: e.memset(FAC[:, :, :], 1.0), reads=[FAC], writes=[FAC])
                                if not rev:
                                    cp('dve', FAC[:, 0, 1::2], SM[:, 40:40 + NS:2], [SM, FAC], [FAC])
                                    cp('dve', FAC[:, 1, 0::2], SM[:, 41:40 + NS:2], [SM, FAC], [FAC])
                                    act(e1[:, :], gi[:, :], AF.Exp, [gi], [e1])
                                    act(e2_[:, :], lf[:, :], AF.Exp, [lf], [e2_], scale=-1.0)
                                else:
                                    cp('dve', FAC[:, 0, 0::2], SM[:, 41:40 + NS:2], [SM, FAC], [FAC])
                                    cp('dve', FAC[:, 1, 1::2], SM[:, 40:40 + NS:2], [SM, FAC], [FAC])
                                    act(e1[:, :], lf[:, :], AF.Exp, [lf], [e1], scale=-1.0)
                                    act(e2_[:, :], ge[:, :], AF.Exp, [ge], [e2_])
                                act(sg_[:, :], sg_[:, :], AF.Exp, [sg_], [sg_])
                                fq = FAC[:, 0, :].unsqueeze(2).to_broadcast([128, NS, 32])
                                fk = FAC[:, 1, :].unsqueeze(2).to_broadcast([128, NS, 32])
                                bq = proj_fm(1664 + i * 128, False)
                                tt('dve', QT[i][:, :], psl(bq, 0, NB), e1[:, :], ALU.mult, [bank(bq), e1], [QT[i]])
                                tt('pool', KH[i][:, :], f_[:, :], e2_[:, :], ALU.mult, [f_, e2_], [KH[i]])
                                dsc = M1p[:, 0:2, :].rearrange("p a t -> p (a t)")
                                tt('dve', dsc, psl(bq, 0, NB), f_[:, :], ALU.mult, [bank(bq), f_], [M1p])
                                for t_ in range(NT):
                                    mm(PS[:, 3072 + 340 + t_ * 8 + 2 * i: 3072 + 340 + t_ * 8 + 2 * i + 2], M1p[:, 0:2, :].rearrange("p a t -> p (a t)")[:, t_ * 128:(t_ + 1) * 128], E2[:, :], True, True,
                                       [M1p, E2], [bank(6)], inc=(t_ == NT - 1))
                                if not rev:
                                    tt('pool', QX[i][:, :], f_[:, :], sg_[:, :], ALU.mult, [f_, sg_], [QX[i]])
                                    tt('dve', v32(QI[i]), v32(QT[i]), fq, ALU.mult, [QT[i], FAC], [QI[i]])
                                    tt('pool', v32(KS[i]), v32(QX[i]), fk, ALU.mult, [QX[i], FAC], [KS[i]])
                                else:
                                    tt('dve', QX[i][:, :], psl(bq, 0, NB), sg_[:, :], ALU.mult, [bank(bq), sg_], [QX[i]])
                                    tt('dve', v32(QI[i]), v32(QX[i]), fq, ALU.mult, [QX[i], FAC], [QI[i]])
                                    tt('pool', v32(KS[i]), v32(KH[i]), fk, ALU.mult, [KH[i], FAC], [KS[i]])
                            if rev:
                                cp('act', SH1[:, jt0:jt0 + NT, :], PS[:, 3072 + 340:3072 + 340 + NT * 8].rearrange("p (t c) -> p t c", c=8)[:, :, 0:6], [bank(6)], [SH1])
                            else:
                                cp('act', SH0[:, :, :], PS[:, 3072 + 340:3072 + 340 + NT * 8].rearrange("p (t c) -> p t c", c=8)[:, :, 0:6], [bank(6)], [SH0])
                            if first_dbg:
                                dbg('qt0' + tag, QT[0][:, :], [128, NB], BF16, [QT[0]])
                                dbg('kh0' + tag, KH[0][:, :], [128, NB], BF16, [KH[0]])

                            stop_at('hprep')
                            for t_ in range(NT):
                                for k in range(8):
                                    mm(psl(2, 0, 384), XN[:, k, 1 + t_ * 128: 1 + (t_ + 1) * 128], WIN[:, k, 1664 + 384:1664 + 768], k == 0, k == 7, [XN, WIN], [bank(2)], inc=(k == 7))
                                cp('act', I_tm[:, t_, :], psl(2, 0, 384), [bank(2)], [I_tm])
                            for t_ in range(NT):
                                for gi_, (srcs, dst) in enumerate(((V_b, V_tm), (KT, K_tm), (BN, B_tm), (KS, KH_tm))):
                                    for i in range(3):
                                        tr(PST[:, (gi_ % 2) * 512 + i * 128:(gi_ % 2) * 512 + (i + 1) * 128], srcs[i][:, t_ * 128:(t_ + 1) * 128], [srcs[i]], PSTK, inc=(i == 2))
                                    cp('dve' if gi_ % 2 == 0 else 'act', dst[:, t_, :], PST[:, (gi_ % 2) * 512:(gi_ % 2) * 512 + 384], PSTK, [dst])
                            if first_dbg:
                                dbg('vtm' + tag, V_tm[:, :, :], [128, NT, 384], BF16, [V_tm])
                                dbg('itm' + tag, I_tm[:, :, :], [128, NT, 384], BF16, [I_tm])

                            stop_at('trans')
                            tts = list(range(NT))
                            if rev:
                                tts = tts[::-1]
                            mk_ = lambda m_: MASKS[:, m_, :].unsqueeze(1).unsqueeze(1).to_broadcast([128, 2, 3, 128])
                            idn_bc = IDN[:, :].unsqueeze(1).unsqueeze(1).to_broadcast([128, 2, 3, 128])
                            if not rev:
                                mA, mP, mX0, mT0, mT0T = mk_(0), mk_(1), mk_(2), mk_(4), mk_(5)
                            else:
                                mA, mP, mX0, mT0, mT0T = mk_(2), mk_(3), mk_(0), mk_(5), mk_(4)
                            BIGA, BIGB, BIGC = (3, 4), (5, 6), (0, 1)

                            def bigh(reg, h, w=128):
                                c0 = reg[h % 2] * 512 + (h // 2) * w
                                return PS[:, c0:c0 + w]

                            def big(reg, w=128, rows=slice(0, 128)):
                                v = PS[rows, reg[0] * 512:(reg[0] + 2) * 512].rearrange("p (b c) -> p b c", b=2)[:, :, 0:3 * w]
                                return v.rearrange("p b (j w) -> p b j w", j=3)

                            def sbv(buf_ap):
                                return buf_ap.rearrange("p (j b) w -> p b j w", b=2)

                            def bigk(reg):
                                return [bank(reg[0]), bank(reg[1])]

                            def hsl(bufs, h, t_):
                                return bufs[h // 2][(h % 2) * 64:(h % 2) * 64 + 64, t_ * 128:(t_ + 1) * 128]

                            for t_ in tts:
                                tsl = slice(t_ * 128, (t_ + 1) * 128)
                                for h in range(6):
                                    mm(bigh(BIGA, h), hsl(KT, h, t_), hsl(KP, h, t_), True, True, [KT[h // 2], KP[h // 2]], bigk(BIGA), inc=(h == 5))
                                tt('dve', sbv(AT_sb[:, :, :]), big(BIGA), mA, ALU.mult, bigk(BIGA) + [MASKS], [AT_sb])
                                stop_at('inv1')
                                if emit:
                                    for h in range(6):
                                        mm(bigh(BIGB, h), hsl(KT, h, t_), hsl(RT, h, t_), True, True, [KT[h // 2], RT[h // 2]], bigk(BIGB), inc=(h == 5))
                                    tt('dve', sbv(PT_sb[:, :, :]), big(BIGB), mP, ALU.mult, bigk(BIGB) + [MASKS], [PT_sb])
                                for h in range(6):
                                    mm(bigh(BIGA, h), hsl(BN, h, t_), hsl(KP, h, t_), True, True, [BN[h // 2], KP[h // 2]], bigk(BIGA), inc=(h == 5))
                                tt('dve', sbv(XTb[:, :, :]), big(BIGA), mA, ALU.mult, bigk(BIGA) + [MASKS], [XTb])
                                tt('dve', sbv(TTb[:, :, :]), big(BIGA), mT0T, ALU.mult, bigk(BIGA) + [MASKS], [TTb])
                                tt('dve', sbv(TTb[:, :, :]), sbv(TTb[:, :, :]), idn_bc, ALU.add, [TTb, IDN], [TTb])
                                if emit:
                                    for h in range(6):
                                        mm(bigh(BIGB, h), hsl(BN, h, t_), hsl(RT, h, t_), True, True, [BN[h // 2], RT[h // 2]], bigk(BIGB), inc=(h == 5))
                                    tt('dve', sbv(QN_sb[:, :, :]), big(BIGB), mP, ALU.mult, bigk(BIGB) + [MASKS], [QN_sb])
                                for h in range(6):
                                    mm(bigh(BIGB, h), hsl(KP, h, t_), hsl(BN, h, t_), True, True, [KP[h // 2], BN[h // 2]], bigk(BIGB), inc=(h == 5))
                                tt('dve', sbv(XA[:, :, :]), big(BIGB), mX0, ALU.mult, bigk(BIGB) + [MASKS], [XA])
                                tt('dve', sbv(TNb[:, :, :]), big(BIGB), mT0, ALU.mult, bigk(BIGB) + [MASKS], [TNb])
                                tt('dve', sbv(TNb[:, :, :]), sbv(TNb[:, :, :]), idn_bc, ALU.add, [TNb, IDN], [TNb])
                                if emit:
                                    for h in range(6):
                                        mm(bigh(BIGA, h), hsl(KH, h, t_), hsl(QT, h, t_), True, True, [KH[h // 2], QT[h // 2]], bigk(BIGA), inc=(h == 5))
                                    for h in range(6):
                                        if not rev:
                                            mm(bigh(BIGC, h), hsl(QX, h, t_), hsl(QT, h, t_), True, True, [QX[h // 2], QT[h // 2]], bigk(BIGC), inc=(h == 5))
                                        else:
                                            mm(bigh(BIGC, h), hsl(KH, h, t_), hsl(QX, h, t_), True, True, [KH[h // 2], QX[h // 2]], bigk(BIGC), inc=(h == 5))
                                    tt('dve', sbv(PH_sb[:, :, :]), big(BIGA), mk_(16 if not rev else 17), ALU.mult, bigk(BIGA) + [MASKS], [PH_sb])
                                    tt('dve', sbv(M1b[:, :, :]), big(BIGC), mk_(18 if not rev else 19), ALU.mult, bigk(BIGC) + [MASKS], [M1b])
                                    tt('pool', PH_sb[:, :, :], PH_sb[:, :, :], M1b[:, :, :], ALU.add, [PH_sb, M1b], [PH_sb])
                                stop_at('inv2')
                                for lev in range(1, 6):
                                    mk = mk_((6 if not rev else 11) + lev - 1)
                                    mkT = mk_((11 if not rev else 6) + lev - 1)
                                    for h in range(6):
                                        mm(bigh(BIGA, h), XTb[:, h, :], TNb[:, h, :], True, True, [XTb, TNb], bigk(BIGA), inc=(h == 5))
                                    for h in range(6):
                                        mm(bigh(BIGB, h), XA[:, h, :], TTb[:, h, :], True, True, [XA, TTb], bigk(BIGB), inc=(h == 5))
                                    tt('dve', sbv(M1b[:, :, :]), big(BIGA), mk, ALU.mult, bigk(BIGA) + [MASKS], [M1b])
                                    tt('dve', sbv(M1p[:, :, :]), big(BIGB), mkT, ALU.mult, bigk(BIGB) + [MASKS], [M1p])
                                    for h in range(6):
                                        mm(bigh(BIGA, h), TTb[:, h, :], M1b[:, h, :], True, True, [TTb, M1b], bigk(BIGA), inc=(h == 5))
                                    for h in range(6):
                                        mm(bigh(BIGC, h), TNb[:, h, :], M1p[:, h, :], True, True, [TNb, M1p], bigk(BIGC), inc=(h == 5))
                                    tt('dve', sbv(TNb[:, :, :]), big(BIGA), sbv(TNb[:, :, :]), ALU.add, bigk(BIGA) + [TNb], [TNb])
                                    tt('dve', sbv(TTb[:, :, :]), big(BIGC), sbv(TTb[:, :, :]), ALU.add, bigk(BIGC) + [TTb], [TTb])
                                TTf = TTb
                                if first_dbg and t_ == 0:
                                    dbg('at' + tag, AT_sb[:, :, :], [128, 6, 128], BF16, [AT_sb])
                                    dbg('ttf' + tag, TTf[:, :, :], [128, 6, 128], BF16, [TTf])

                                stop_at('inv')
                                halves = (1, 0) if rev else (0, 1)
                                for half in halves:
                                    c_ = t_ * 2 + half
                                    hp = slice(half * 64, half * 64 + 64)
                                    gam_bc = GAM[:, :, c_:c_ + 1].to_broadcast([128, 3, 128])
                                    gamh_bc = GAMH[:, :, c_:c_ + 1].to_broadcast([128, 3, 128])
                                    H3 = lambda hb: hb[:, :].rearrange("p (a v) -> p a v", a=3)
                                    if rev:
                                        tt('dve', H3(HR32), H3(HR32), gam_bc, ALU.mult, [HR32, GAM], [HR32])
                                        cp('act', HR16[:, :], HR32[:, :], [HR32], [HR16])

                                    def hop(hb16, h):
                                        pr = h // 2
                                        o = (h % 2) * 64
                                        return hb16[o:o + 64, pr * 128 + o: pr * 128 + o + 64]
                                    RY, YH = (0, 1), (3, 4)
                                    h64 = lambda ap: ap.rearrange("p (h v) -> p h v", h=6)
                                    for h in range(6):
                                        mm(bigh(RY, h, 64), hsl(KP, h, t_), hop(HR16, h), True, False, [KP[h // 2], HR16], bigk(RY), inc=False)
                                        mm(bigh(RY, h, 64), AT_sb[:, h, :], V_tm[:, t_, h * 64:(h + 1) * 64], False, True, [AT_sb, V_tm], bigk(RY), inc=(h == 5))
                                    cp('dve', sbv(h64(R_sb[hp, :])), big(RY, 64, hp), bigk(RY), [R_sb])
                                    if emit:
                                        for h in range(6):
                                            mm(bigh(YH, h, 64), hsl(QI, h, t_), hop(HH16, h), True, False, [QI[h // 2], HH16], bigk(YH), inc=False)
                                            mm(bigh(YH, h, 64), PH_sb[:, h, :], I_tm[:, t_, h * 64:(h + 1) * 64], False, True, [PH_sb, I_tm], bigk(YH), inc=(h == 5))
                                    for pr in range(3):
                                        mm(psl(6, pr * 128, 128), KH_tm[hp, t_, pr * 128:(pr + 1) * 128], I_tm[hp, t_, pr * 128:(pr + 1) * 128], True, True, [KH_tm, I_tm], [bank(6)], inc=(pr == 2))
                                    for h in range(6):
                                        mm(psl(2, h * 64, 64), TTf[hp, h, :], R_sb[hp, h * 64:(h + 1) * 64], True, True, [TTf, R_sb], [bank(2)], inc=(h == 5))
                                    cp('act', U_sb[hp, :], PS[hp, 1024:1024 + 384], [bank(2)], [U_sb])
                                    if emit:
                                        for h in range(6):
                                            mm(bigh(RY, h, 64), hsl(RT, h, t_), hop(HR16, h), True, False, [RT[h // 2], HR16], bigk(RY), inc=False)
                                            mm(bigh(RY, h, 64), PT_sb[:, h, :], V_tm[:, t_, h * 64:(h + 1) * 64], False, False, [PT_sb, V_tm], bigk(RY), inc=False)
                                            mm(bigh(RY, h, 64), QN_sb[:, h, :], U_sb[:, h * 64:(h + 1) * 64], False, True, [QN_sb, U_sb], bigk(RY), inc=(h == 5))
                                    for pr in range(3):
                                        mm(psl(5, pr * 128, 128), K_tm[hp, t_, pr * 128:(pr + 1) * 128], V_tm[hp, t_, pr * 128:(pr + 1) * 128], True, False, [K_tm, V_tm], [bank(5)], inc=False)
                                        mm(psl(5, pr * 128, 128), B_tm[hp, t_, pr * 128:(pr + 1) * 128], U_sb[hp, pr * 128:(pr + 1) * 128], False, True, [B_tm, U_sb], [bank(5)], inc=(pr == 2))
                                    if emit:
                                        if rev:
                                            cp('act', sbv(h64(YST[hp, t_, :])), big(RY, 64, hp), bigk(RY), [YST])
                                            cp('act', sbv(h64(OST[hp, t_, :])), big(YH, 64, hp), bigk(YH), [OST])
                                        else:
                                            tt('dve', sbv(h64(Y_sb[hp, t_, :])), big(RY, 64, hp), sbv(h64(YBT[hp, t_, :])), ALU.add, bigk(RY) + [YBT], [Y_sb])
                                            tt('dve', sbv(h64(O_sb[hp, t_, :])), big(YH, 64, hp), sbv(h64(OBT[hp, t_, :])), ALU.add, bigk(YH) + [OBT], [O_sb])
                                    tt('dve', HR32[:, :], HR32[:, :], psl(5, 0, 384), ALU.add, [HR32, bank(5)], [HR32])
                                    tt('pool', H3(HH32), H3(HH32), gamh_bc, ALU.mult, [HH32, GAMH], [HH32])
                                    tt('dve', HH32[:, :], HH32[:, :], psl(6, 0, 384), ALU.add, [HH32, bank(6)], [HH32])
                                    cp('pool', HH16[:, :], HH32[:, :], [HH32], [HH16])
                                    if not rev:
                                        tt('pool', H3(HR32), H3(HR32), gam_bc, ALU.mult, [HR32, GAM], [HR32])
                                        cp('act', HR16[:, :], HR32[:, :], [HR32], [HR16])
                                if first_dbg and t_ == (0 if rev else NT - 1):
                                    dbg('u' + tag, U_sb[:, :], [128, 384], BF16, [U_sb])
                                    dbg('hr' + tag, HR32[:, :], [128, 384], F32, [HR32])
                                    dbg('hh' + tag, HH32[:, :], [128, 384], F32, [HH32])

                            if rev:
                                if emit:
                                    for t_ in range(NT):
                                        dma(YB[jt0 + t_], YST[:, t_, :], [YST], [('YB', jt0 + t_)])
                                        dma(OB[jt0 + t_], OST[:, t_, :], [OST], [('OB', jt0 + t_)])
                                    if first_dbg:
                                        dbg('yst', YST[:, :, :], [128, NT, 384], BF16, [YST])
                                        dbg('ost', OST[:, :, :], [128, NT, 384], BF16, [OST])
                                stop_at('chunk')
                                continue
                            if not emit:
                                continue
                            if first_dbg:
                                dbg('ysb', Y_sb[:, :, :], [128, NT, 384], BF16, [Y_sb])
                                dbg('osb', O_sb[:, :, :], [128, NT, 384], BF16, [O_sb])

                            for t_ in range(NT):
                                xcol = 1 + t_ * 128
                                for k in range(8):
                                    mm(psl(2, 0, 256), XN[:, k, xcol:xcol + 128], WIN[:, k, 0:256], k == 0, k == 7, [XN, WIN], [bank(2)], inc=(k == 7))
                                cp('act', PV_tm[:, t_, :], psl(2, 0, 256), [bank(2)], [PV_tm])
                            for t_ in range(NT):
                                jg = jt0 + t_
                                xcol = 1 + t_ * 128
                                for hh in range(2):
                                    for k in range(8):
                                        mm(psl(3 + hh, 0, 512), XN[:, k, xcol:xcol + 128], WIN[:, k, 3200 + hh * 512:3200 + (hh + 1) * 512], k == 0, k == 7, [XN, WIN], [bank(3 + hh)], inc=(k == 7))
                                    act(SG[:, hh * 512:(hh + 1) * 512], psl(3 + hh, 0, 512), AF.Silu, [bank(3 + hh)], [SG])
                                for ct in range(2):
                                    if is_ctx:
                                        for st_ in range(NT):
                                            mm(psl(2, 0, 256), PV_tm[:, st_, ct * 128:(ct + 1) * 128],
                                               C['pool_ctx'][:, st_, 2 * ct:2 * ct + 2, t_ * 128:(t_ + 1) * 128], st_ == 0, st_ == NT - 1, [PV_tm, C['pool_ctx']], [bank(2)], inc=(st_ == NT - 1))
                                    else:
                                        mm(psl(2, 0, 256), PV_tm[:, t_, ct * 128:(ct + 1) * 128], C['pool_lat'][:, 2 * ct:2 * ct + 2, :], True, True, [PV_tm, C['pool_lat']], [bank(2)])
                                    cp('dve', PLT[:, ct, 0:256], psl(2, 0, 256), [bank(2)], [PLT])
                                for g_ in range(4):
                                    ct, gh = g_ // 2, g_ % 2
                                    mm(psl(5 + gh, ct * 64, 64), PLT[gh * 64:gh * 64 + 64, ct, gh * 128:gh * 128 + 128], PW[gh * 64:gh * 64 + 64, ct, :], True, True, [PLT, PW], [bank(5), bank(6)], inc=(g_ == 3))
                                w0_, w1_ = W384
                                w2_ = w0_
                                tt('dve', w0_[:, 0:256].rearrange("p (ct gh o) -> p gh ct o", ct=2, gh=2), PS[:, 5 * 512:7 * 512].rearrange("p (gh c) -> p gh c", gh=2)[:, :, 0:128].rearrange("p gh (ct o) -> p gh ct o", ct=2),
                                   PSC[:, :].rearrange("p (ct gh o) -> p gh ct o", ct=2, gh=2), ALU.mult, [bank(5), bank(6), PSC], [w0_])
                                tt('pool', MIX[:, 0:256], w0_[:, 0:256], SG[:, 0:256], ALU.mult, [w0_, SG], [MIX])
                                y3 = Y_sb[:, t_, :].rearrange("p (h v) -> p h v", h=6)
                                S.op('dve', lambda e, y3=y3: e.tensor_reduce(out=SM[:, 8:14], in_=y3, axis=AX.X, op=ALU.add), reads=[Y_sb, SM], writes=[SM])
                                act(w0_[:, :], Y_sb[:, t_, :], AF.Square, [Y_sb], [w0_])
                                S.op('dve', lambda e, w0_=w0_: e.tensor_reduce(out=SM[:, 14:20], in_=w0_[:, :].rearrange("p (h v) -> p h v", h=6), axis=AX.X, op=ALU.add), reads=[w0_, SM], writes=[SM])
                                ts('dve', SM[:, 8:14], SM[:, 8:14], 1.0 / 64, None, ALU.mult, None, [SM], [SM])
                                tt('dve', SM[:, 20:26], SM[:, 8:14], SM[:, 8:14], ALU.mult, [SM], [SM])
                                stt(SM[:, 14:20], SM[:, 14:20], 1.0 / 64, SM[:, 20:26], ALU.mult, ALU.subtract, [SM], [SM])
                                act(SM[:, 14:20], SM[:, 14:20], AF.Sqrt, [SM], [SM], bias=GN_EPS)
                                S.op('dve', lambda e: e.reciprocal(out=SM[:, 14:20], in_=SM[:, 14:20]), reads=[SM], writes=[SM])
                                w13 = w1_[:, :].rearrange("p (h v) -> p h v", h=6)
                                tt('dve', w13, y3, SM[:, 8:14].unsqueeze(2).to_broadcast([128, 6, 64]), ALU.subtract, [Y_sb, SM], [w1_])
                                tt('dve', w13, w13, SM[:, 14:20].unsqueeze(2).to_broadcast([128, 6, 64]), ALU.mult, [w1_, SM], [w1_])
                                tt('pool', w1_[:, :], w1_[:, :], GNG[:, :], ALU.mult, [w1_, GNG], [w1_])
                                tt('pool', w1_[:, :], w1_[:, :], GNB[:, :], ALU.add, [w1_, GNB], [w1_])
                                tt('dve', SM[:, 26:32], S0[:, t_, :], S1[:, jg, :], ALU.add, [S0, S1, SM], [SM])
                                tt('dve', w0_[:, :].rearrange("p (h v) -> p h v", h=6), V_tm[:, t_, :].rearrange("p (h v) -> p h v", h=6),
                                   SM[:, 26:32].unsqueeze(2).to_broadcast([128, 6, 64]), ALU.mult, [V_tm, SM], [w0_])
                                tt('pool', w1_[:, :], w1_[:, :], w0_[:, :], ALU.add, [w1_, w0_], [w1_])
                                tt('dve', MIX[:, 256:640], w1_[:, :], SG[:, 256:640], ALU.mult, [w1_, SG], [MIX])
                                tt('dve', SM[:, 32:38], SH0[:, t_, :], SH1[:, jg, :], ALU.add, [SH0, SH1, SM], [SM])
                                o3 = w1_[:, :].rearrange("p (h v) -> p h v", h=6)
                                tt('dve', o3, I_tm[:, t_, :].rearrange("p (h v) -> p h v", h=6), SM[:, 32:38].unsqueeze(2).to_broadcast([128, 6, 64]), ALU.mult, [I_tm, SM], [w1_])
                                tt('pool', w1_[:, :], w1_[:, :], O_sb[:, t_, :], ALU.add, [w1_, O_sb], [w1_])
                                act(w2_[:, :], w1_[:, :], AF.Square, [w1_], [w2_])
                                S.op('dve', lambda e, w2_=w2_: e.tensor_reduce(out=SM[:, 20:26], in_=w2_[:, :].rearrange("p (h v) -> p h v", h=6), axis=AX.X, op=ALU.add), reads=[w2_, SM], writes=[SM])
                                act(SM[:, 20:26], SM[:, 20:26], AF.Sqrt, [SM], [SM], bias=RMS_EPS, scale=1.0 / 64)
                                S.op('dve', lambda e: e.reciprocal(out=SM[:, 20:26], in_=SM[:, 20:26]), reads=[SM], writes=[SM])
                                tt('dve', w2_[:, :].rearrange("p (h v) -> p h v", h=6), o3, SM[:, 20:26].unsqueeze(2).to_broadcast([128, 6, 64]), ALU.mult, [w1_, SM], [w2_])
                                tt('pool', w2_[:, :], w2_[:, :], HGG[:, :], ALU.mult, [w2_, HGG], [w2_])
                                tt('dve', MIX[:, 640:1024], w2_[:, :], SG[:, 640:1024], ALU.mult, [w2_, SG], [MIX])
                                if first_dbg and t_ == 0:
                                    dbg('mix', MIX[:, :], [128, D], BF16, [MIX])
                                for k in range(8):
                                    tr(PST[:, k * 128:(k + 1) * 128], MIX[:, k * 128:(k + 1) * 128], [MIX], PSTK, inc=(k == 7))
                                cp('act', MIXT[:, :, :], PST[:, :].rearrange("p (k t) -> p k t", k=8), PSTK, [MIXT])
                                for hh in range(2):
                                    for k in range(8):
                                        mm(psl(3 + hh, 0, 512), MIXT[:, k, :], WOUT[:, k, hh * 512:(hh + 1) * 512], k == 0, k == 7, [MIXT, WOUT], [bank(3 + hh)], inc=(k == 7))
                                dma(HT[:, 0:D], h_in[jg * 128:(jg + 1) * 128, :], ['H1'] if l > 0 else [], [HT])
                                gt_t = GT
                                tt('dve', T1[:, 0:D], PS[:, 3 * 512:3 * 512 + D], gt_t[:, :], ALU.mult, [bank(3), bank(4), gt_t], [T1])
                                stt(T1[:, 0:D], HT[:, 0:D], ALPHA, T1[:, 0:D], ALU.mult, ALU.add, [HT, T1], [T1])
                                for hh in range(2):
                                    S.op('dve', lambda e, hh=hh: e.bn_stats(out=ST6[:, hh, :], in_=T1[:, hh * 512:(hh + 1) * 512]), reads=[T1, ST6], writes=[ST6])
                                S.op('dve', lambda e: e.bn_aggr(out=SM[:, 0:2], in_=ST6[:, :, :]), reads=[ST6], writes=[SM])
                                act(SM[:, 2:3], SM[:, 1:2], AF.Sqrt, [SM], [SM], bias=LN_EPS)
                                S.op('dve', lambda e: e.reciprocal(out=SM[:, 3:4], in_=SM[:, 2:3]), reads=[SM], writes=[SM])
                                ts('dve', SM[:, 4:5], SM[:, 0:1], SM[:, 3:4], -1.0, ALU.mult, ALU.mult, [SM], [SM])
                                act(T1[:, 0:D], T1[:, 0:D], AF.Identity, [T1, SM], [T1], bias=SM[:, 4:5], scale=SM[:, 3:4])
                                tt('pool', T1[:, 0:D], T1[:, 0:D], LNG[:, :], ALU.mult, [T1, LNG], [T1])
                                tt('dve', T1[:, 0:D], T1[:, 0:D], LNB[:, :], ALU.add, [T1, LNB], [T1])
                                if last:
                                    r0 = (jg - 2) * 128
                                    S.out_tokens.append(dma(out_d[b, r0:r0 + 128, :], T1[:, 0:D], [T1], []))
                                else:
                                    dma(H1[b, jg * 128:(jg + 1) * 128, :], T1[:, 0:D], [T1], ['H1'])
                                    if first_dbg and t_ == 0:
                                        dbg('hout', T1[:, 0:D], [128, D], F32, [T1])
                            stop_at('out')
        except _Stop:
            pass
        S.finish()
        S.emit()
        n_instr = S.n_instr
    return nc, dbg_outs, n_instr


_CACHE = {}


def host_layout(inputs, core):
    f32 = np.float32
    g = lambda k: np.asarray(inputs[k], dtype=f32)
    b0 = core * NB_CORE
    m = {}
    m['hin'] = np.ascontiguousarray(np.concatenate([g('ctx')[b0:b0 + NB_CORE], g('x')[b0:b0 + NB_CORE]], axis=1))
    cv = np.stack([g('c')[b0], g('c')[b0 + 1], g('c_ctx')])
    m['cT'] = np.ascontiguousarray(cv.reshape(3, 8, 128).transpose(2, 1, 0))
    return m


def host_shared(inputs):
    f32 = np.float32
    g = lambda k: np.asarray(inputs[k], dtype=f32)
    m = {}
    for k in ('mod_w', 'mod_b', 'w_in', 'w_out', 'ln_g', 'ln_b'):
        m[k] = np.ascontiguousarray(g(k))
    m['w_up'] = np.ascontiguousarray(g('rwkv_w_up').reshape(DEPTH, 128, 384))
    m['a_up'] = np.ascontiguousarray(g('rwkv_a_up').reshape(DEPTH, 128, 384))
    pw = g('pool_w').reshape(DEPTH, 2, 2, 64, 64)
    m['pw'] = np.ascontiguousarray(pw.transpose(0, 2, 3, 1, 4).reshape(DEPTH, 128, 2, 64))
    pp = np.zeros((DEPTH, 128, NPP), f32)
    sh = g('rwkv_shift')
    for l in range(DEPTH):
        pp[l, :, 0:33] = sh[l].reshape(3, 11, 128).transpose(2, 1, 0).reshape(128, 33)
        pp[l, :, 33:39] = g('rwkv_w0')[l].reshape(2, 3, 128).transpose(2, 0, 1).reshape(128, 6)
        pp[l, :, 39:45] = g('rwkv_a0')[l].reshape(2, 3, 128).transpose(2, 0, 1).reshape(128, 6)
        pp[l, :, 45:48] = g('rwkv_k_k')[l].reshape(3, 128).T
        pp[l, :, 48:51] = g('rwkv_k_a')[l].reshape(3, 128).T
        pp[l, :, 51:57] = g('rwkv_r_k')[l].reshape(2, 3, 128).transpose(2, 0, 1).reshape(128, 6)
        lbl = g('hgrn_lb_logits')
        pp[l, :, 57:69] = lbl.reshape(2, DEPTH, 3, 128).transpose(3, 0, 1, 2).reshape(128, 12)
    m['pp'] = pp
    m['gn_g'] = np.ascontiguousarray(g('rwkv_gn_g'))
    m['gn_b'] = np.ascontiguousarray(g('rwkv_gn_b'))
    m['hg_g'] = np.ascontiguousarray(g('hgrn_norm_g'))
    m['pscale'] = np.ascontiguousarray(g('pool_scale'))
    for k, v in make_consts().items():
        m['c_' + k] = v
    return m


def kernel(**inputs):
    if 'nc' not in _CACHE:
        _CACHE['nc'] = build_program()[0]
    nc = _CACHE['nc']
    shared = host_shared(inputs)
    in_maps = []
    for core in range(8):
        m = dict(shared)
        m.update(host_layout(inputs, core))
        in_maps.append(m)
    res = run_bass_kernel_spmd(nc, in_maps, core_ids=list(range(8)))
    out = np.concatenate([np.asarray(r["out"], dtype=np.float32) for r in res.results], axis=0)
    return out
```

```python
import numpy as np
import ml_dtypes
from contextlib import ExitStack
import concourse.bass as bass
import concourse.mybir as mybir
from concourse.bass_utils import run_bass_kernel_spmd

F32 = mybir.dt.float32
BF16 = mybir.dt.bfloat16
AF = mybir.ActivationFunctionType
ALU = mybir.AluOpType
AX = mybir.AxisListType

D = 1024
NB_CORE = 2
CTX = 256
SEQ = 2048
TOK = CTX + SEQ
NTT = TOK // 128
DEPTH = 2
DIN = 4224
ALPHA = float((2 * DEPTH) ** 0.25)
LN_EPS = 1e-5
GN_EPS = 64e-5
RMS_EPS = 1e-6
CDEC = float(np.exp(-0.5))
NB = 256
NPP = 69
XW = 2308
SEM_LIMIT = 30000
N_DMA_SLOTS = 12


class Buf:
    def __init__(self, name, t):
        self.name = name
        self.t = t

    def __getitem__(self, idx):
        return self.t[idx]


class Sched:
    ENGS = ('pe', 'act', 'dve', 'pool', 'sp')

    def __init__(self, nc, stack):
        self.nc = nc
        self.stack = stack
        self.prog = {e: [] for e in self.ENGS}
        self.cnt = {e: 0 for e in self.ENGS}
        self.sem = {}
        self.nsem = 0
        for e in self.ENGS:
            self.sem[e] = self._newsem(e)
        self.known = {e: {} for e in self.ENGS}
        self.snap = {}
        self.track = {}
        self.dma_slots = {q: [[self._newsem('d%s' % q), 0] for _ in range(N_DMA_SLOTS)] for q in ('sp', 'pool', 'act')}
        self.dma_rr = {q: 0 for q in ('sp', 'pool', 'act')}
        self.n_instr = 0
        self.epoch_done = {e: {} for e in self.ENGS}
        self.out_tokens = []

    def _newsem(self, tag):
        self.nsem += 1
        return self.stack.enter_context(self.nc.semaphore('s_%s_%d' % (tag, self.nsem)))

    def sbuf(self, name, shape, dtype):
        return Buf(name, self.stack.enter_context(self.nc.sbuf_tensor(name, list(shape), dtype)))

    def psum(self, name, shape, dtype):
        return Buf(name, self.stack.enter_context(self.nc.psum_tensor(name, list(shape), dtype)))

    def _need(self, eng, tok, waits):
        if tok is None:
            return
        sem, val = tok
        if self.known[eng].get(sem, 0) >= val:
            return
        if waits.get(sem, 0) < val:
            waits[sem] = val

    def _deps(self, eng, reads, writes):
        waits = {}
        for k in reads:
            tr = self.track.get(k)
            if tr is not None:
                self._need(eng, tr[0], waits)
        for k in writes:
            tr = self.track.get(k)
            if tr is not None:
                self._need(eng, tr[0], waits)
                for s, v in tr[1].items():
                    self._need(eng, (s, v), waits)
        out = {}
        for s, v in waits.items():
            if eng == 'pe' and s is self.sem['pe']:
                continue
            out[s] = v
        kn = self.known[eng]
        for s, v in out.items():
            if kn.get(s, 0) < v:
                kn[s] = v
            sn = self.snap.get((s, v))
            if sn:
                for s2, v2 in sn.items():
                    if kn.get(s2, 0) < v2:
                        kn[s2] = v2
        return list(out.items())

    def _record(self, tok, reads, writes):
        s, v = tok
        for k in reads:
            tr = self.track.setdefault(k, [None, {}])
            if tr[1].get(s, 0) < v:
                tr[1][s] = v
        for k in writes:
            self.track[k] = [tok, {}]

    def op(self, eng, fn, reads=(), writes=(), inc=True):
        reads = [r for r in reads if r is not None]
        waits = self._deps(eng, reads, writes)
        if inc and self.cnt[eng] >= SEM_LIMIT:
            self.epoch_done[eng][self.sem[eng]] = self.cnt[eng]
            self.sem[eng] = self._newsem(eng)
            self.cnt[eng] = 0
        sem = self.sem[eng]
        if inc:
            self.cnt[eng] += 1
        tok = (sem, self.cnt[eng] if inc else self.cnt[eng] + 1)
        self.prog[eng].append((fn, waits, (sem, 1) if inc else None))
        if inc:
            sn = dict(self.known[eng])
            sn.update(self.epoch_done[eng])
            self.snap[tok] = sn
        self._record(tok, reads, writes)
        self.n_instr += 1
        return tok

    def dma(self, q, out_ap, in_ap, reads=(), writes=(), **kw):
        reads = [r for r in reads if r is not None]
        slots = self.dma_slots[q]
        i = self.dma_rr[q]
        self.dma_rr[q] = (i + 1) % len(slots)
        slot = slots[i]
        waits = dict(self._deps(q, reads, writes))
        if slot[1] > 0 and self.known[q].get(slot[0], 0) < slot[1]:
            waits[slot[0]] = slot[1]
            self.known[q][slot[0]] = slot[1]
        slot[1] += 16
        tok = (slot[0], slot[1])
        fn = (lambda e, o=out_ap, i_=in_ap, k=kw: e.dma_start(out=o, in_=i_, **k))
        self.prog[q].append((fn, list(waits.items()), (slot[0], 16)))
        self.snap[tok] = dict(self.known[q])
        self._record(tok, reads, writes)
        self.n_instr += 1
        return tok

    def finish(self, eng='sp'):
        waits = {}
        for s, v in self.out_tokens:
            waits[s] = max(waits.get(s, 0), v)
        self.prog[eng].append((None, list(waits.items()), None))

    def emit(self):
        nc = self.nc
        with nc.Block() as block:
            def run(engobj, items):
                for fn, waits, inc in items:
                    for s, v in waits:
                        engobj.wait_ge(s, v)
                    if fn is None:
                        continue
                    ins = fn(engobj)
                    if inc is not None:
                        ins.then_inc(inc[0], inc[1])

            @block.tensor
            def _(e):
                run(e, self.prog['pe'])

            @block.scalar
            def _(e):
                run(e, self.prog['act'])

            @block.vector
            def _(e):
                run(e, self.prog['dve'])

            @block.gpsimd
            def _(e):
                run(e, self.prog['pool'])

            @block.sync
            def _(e):
                run(e, self.prog['sp'])


def _bf(a):
    return np.ascontiguousarray(a).astype(ml_dtypes.bfloat16)


def make_consts():
    p = np.arange(128)
    same = (p[:, None] // 64) == (p[None, :] // 64)
    a = p[:, None] % 64
    b = p[None, :] % 64
    c = {}
    c['idn'] = _bf(np.eye(128, dtype=np.float32))
    ml = [same & (a < b), same & (a <= b), same & (a > b), same & (a >= b)]
    m0 = same & ((a // 2) == (b // 2)) & ((a % 2) == 1) & ((b % 2) == 0)
    dg = (p[:, None] == p[None, :])
    ml += [m0, m0.T]
    lv = []
    for k in range(1, 6):
        n = 2 ** k
        lv.append(same & ((a // (2 * n)) == (b // (2 * n))) & ((a % (2 * n)) >= n) & ((b % (2 * n)) < n))
    ml += lv
    ml += [m.T for m in lv]
    masks = np.stack(ml).astype(np.float32)
    same32 = (p[:, None] // 32) == (p[None, :] // 32)
    ha = (p[:, None] % 64) < 32
    hb = (p[None, :] % 64) >= 32
    mxf = same & ha & hb
    extra = np.stack([same32 & (p[:, None] < p[None, :]), same32 & (p[:, None] > p[None, :]), mxf, mxf.T]).astype(np.float32)
    masks = np.concatenate([masks, extra], axis=0)
    c['masks'] = _bf(masks.transpose(1, 0, 2))
    c['onesbd'] = _bf(same.astype(np.float32))
    e2 = np.zeros((128, 2), np.float32)
    e2[:64, 0] = 1
    e2[64:, 1] = 1
    c['e2'] = _bf(e2)
    m64 = np.ones((128, NB), np.float32)
    m64[:, ::64] = 0
    c['mask64'] = m64
    m32 = np.ones((128, NB), np.float32)
    m32[:, ::32] = 0
    c['mask32'] = m32

    def pool_mat(n, row):
        Ms = []
        for win in (2, 4, 8, 16):
            left = win // 2
            right = win - 1 - left
            M = np.zeros((n, n), np.float64)
            for t in range(n):
                r0 = (t // row) * row
                tt = t - r0
                lo = max(tt - left, 0)
                hi = min(tt + right, row - 1) + 1
                M[t, r0 + lo:r0 + hi] = 1.0 / (hi - lo)
                M[t, t] -= 1.0
            Ms.append(M)
        return Ms
    lat = pool_mat(128, 64)
    c['pool_lat'] = _bf(np.stack([M.T for M in lat], axis=1).astype(np.float32))
    cx = pool_mat(256, 256)
    pc = np.stack([M.T for M in cx], axis=1).astype(np.float32)
    c['pool_ctx'] = _bf(pc.reshape(2, 128, 4, 256).transpose(1, 0, 2, 3))
    return c


CONST_SHAPES = {'idn': ([128, 128], BF16), 'masks': ([128, 20, 128], BF16), 'onesbd': ([128, 128], BF16),
                'e2': ([128, 2], BF16), 'mask64': ([128, NB], F32), 'mask32': ([128, NB], F32), 'pool_lat': ([128, 4, 128], BF16),
                'pool_ctx': ([128, 2, 4, 256], BF16)}


class _Stop(Exception):
    pass


def build_program(debug=(), depth=DEPTH, nbatch=NB_CORE, stop=None):
    nc = bass.Bass("TRN2", target_bir_lowering=False)
    dram = {}

    def din(name, shape, dt=F32):
        dram[name] = nc.dram_tensor(name, list(shape), dt, kind="ExternalInput").ap()
        return dram[name]

    hin = din("hin", [NB_CORE, TOK, D])
    cT_d = din("cT", [128, 8, 3])
    mod_w = din("mod_w", [DEPTH, D, 3 * D])
    mod_b = din("mod_b", [DEPTH, 3 * D])
    w_in = din("w_in", [DEPTH, D, DIN])
    w_out = din("w_out", [DEPTH, D, D])
    w_up = din("w_up", [DEPTH, 128, 384])
    a_up = din("a_up", [DEPTH, 128, 384])
    pw_d = din("pw", [DEPTH, 128, 2, 64])
    pp_d = din("pp", [DEPTH, 128, NPP])
    gn_g = din("gn_g", [DEPTH, 384])
    gn_b = din("gn_b", [DEPTH, 384])
    hg_g = din("hg_g", [DEPTH, 384])
    pscale = din("pscale", [DEPTH, 256])
    ln_g = din("ln_g", [DEPTH, D])
    ln_b = din("ln_b", [DEPTH, D])
    cd = {k: din("c_" + k, sh, dt) for k, (sh, dt) in CONST_SHAPES.items()}
    out_d = nc.dram_tensor("out", [NB_CORE, SEQ, D], F32, kind="ExternalOutput").ap()
    H1 = nc.dram_tensor("h1_scr", [NB_CORE, TOK, D], F32, kind="Internal").ap()
    MODS = nc.dram_tensor("mods_scr", [3, 3 * D], F32, kind="Internal").ap()
    YB = nc.dram_tensor("yb_scr", [NTT, 128, 384], BF16, kind="Internal").ap()
    OB = nc.dram_tensor("ob_scr", [NTT, 128, 384], BF16, kind="Internal").ap()
    XND = nc.dram_tensor("xn_scr", [128, 8, XW], BF16, kind="Internal").ap()
    dbg_outs = {}

    with ExitStack() as st:
        S = Sched(nc, st)
        sb = S.sbuf

        def dbg(name, ap, shape, dt, reads):
            if name not in debug or name in dbg_outs:
                return
            t = nc.dram_tensor("dbg_" + name, list(shape), dt, kind="ExternalOutput").ap()
            dbg_outs[name] = t
            S.out_tokens.append(S.dma('sp', t, ap, reads=reads))

        WIN = sb("WIN", [128, 8, DIN], BF16)
        WOUT = sb("WOUT", [128, 8, D], BF16)
        XNT = [sb("XNT0", [128, 8, 128], BF16)] * 2
        XBS = [sb("XB%d" % i, [128, 8, NB + 2], BF16) for i in range(2)]
        ZB = sb("ZB", [128, 8, 1], BF16)
        HT = sb("HT", [128, 1056], F32)
        T1 = sb("T1", [128, 1056], F32)
        LNG = sb("LNG", [128, D], F32)
        LNB = sb("LNB", [128, D], F32)
        GT = sb("GT", [128, D], F32)
        GNG = sb("GNG", [128, 384], F32)
        GNB = sb("GNB", [128, 384], F32)
        HGG = sb("HGG", [128, 384], F32)
        PSC = sb("PSC", [128, 256], F32)
        PP = sb("PP", [128, NPP], F32)
        PD = sb("PD", [128, 16], F32)
        WUP = sb("WUP", [128, 384], BF16)
        AUP = sb("AUP", [128, 384], BF16)
        PW = sb("PW", [128, 2, 64], BF16)
        SCSH = sb("SCSH", [128, 3, 2, 8], F32)
        CTs = sb("CTs", [128, 8, 3], F32)
        MODSB = sb("MODSB", [3, 128], F32)
        MODB = sb("MODB", [3, 128], F32)
        S1 = sb("S1", [128, NTT, 6], F32)
        SH1 = sb("SH1", [128, NTT, 6], F32)
        SH0 = sb("SH0", [128, NB // 128, 6], F32)
        C = {k: sb("C_" + k, sh, dt) for k, (sh, dt) in CONST_SHAPES.items()}
        IDN, MASKS, ONESBD, E2, MASK64 = C['idn'], C['masks'], C['onesbd'], C['e2'], C['mask64']
        NT = NB // 128
        R_t = [sb("R_t%d" % i, [128, NB], F32) for i in range(3)]
        K_t = [sb("K_t%d" % i, [128, NB], F32) for i in range(3)]
        V_b = [sb("V_b%d" % i, [128, NB], BF16) for i in range(3)]
        TANHA = sb("TANHA", [128, NB], BF16)
        XAB = sb("XAB", [128, NB], BF16)
        SCR = [sb("SCR%d" % i, [128, NB], F32) for i in range(8)]
        SQB = sb("SQB", [128, NB], BF16)
        RKB = sb("RKB", [128, NB], BF16)
        RT = [sb("RT%d" % i, [128, NB], BF16) for i in range(3)]
        KP = [sb("KP%d" % i, [128, NB], BF16) for i in range(3)]
        KT = [sb("KT%d" % i, [128, NB], BF16) for i in range(3)]
        BN = [sb("BN%d" % i, [128, NB], BF16) for i in range(3)]
        QT = [sb("QT%d" % i, [128, NB], BF16) for i in range(3)]
        KH = [sb("KH%d" % i, [128, NB], BF16) for i in range(3)]
        QX = [XAB] + [sb("QX%d" % i, [128, NB], BF16) for i in range(1, 3)]
        QI = [sb("QI%d" % i, [128, NB], BF16) for i in range(3)]
        KS = [SQB, RKB, TANHA]
        FAC = sb("FAC", [128, 2, NB // 32], F32)
        GAM = sb("GAM", [128, 3, NB // 64], F32)
        GAMH = sb("GAMH", [128, 3, NB // 64], F32)
        V_tm = sb("V_tm", [128, NT, 384], BF16)
        K_tm = sb("K_tm", [128, NT, 384], BF16)
        B_tm = sb("B_tm", [128, NT, 384], BF16)
        KH_tm = sb("KH_tm", [128, NT, 384], BF16)
        I_tm = sb("I_tm", [128, NT, 384], BF16)
        S0 = sb("S0", [128, NT, 6], F32)
        AT_sb = sb("AT_sb", [128, 6, 128], BF16)
        PT_sb = sb("PT_sb", [128, 6, 128], BF16)
        QN_sb = sb("QN_sb", [128, 6, 128], BF16)
        PH_sb = sb("PH_sb", [128, 6, 128], BF16)
        XA = sb("XA", [128, 6, 128], BF16)
        XTb = sb("XTb", [128, 6, 128], BF16)
        TTb = sb("TTb", [128, 6, 128], BF16)
        TNb = sb("TNb", [128, 6, 128], BF16)
        M1b = sb("M1b", [128, 6, 128], BF16)
        M1p = sb("M1p", [128, 6, 128], BF16)
        R_sb = sb("R_sb", [128, 384], BF16)
        U_sb = sb("U_sb", [128, 384], BF16)
        HR32 = sb("HR32", [128, 384], F32)
        HR16 = sb("HR16", [128, 384], BF16)
        HH32 = sb("HH32", [128, 384], F32)
        HH16 = sb("HH16", [128, 384], BF16)
        YST = sb("YST", [128, NT, 384], BF16)
        OST = sb("OST", [128, NT, 384], BF16)
        YBT, OBT, Y_sb, O_sb = YST, OST, YST, OST
        MIX = sb("MIX", [128, D], BF16)
        XH = MIX
        SG = sb("SG", [128, D], BF16)
        MIXT = sb("MIXT", [128, 8, 128], BF16)
        PV_tm = sb("PV_tm", [128, NT, 256], BF16)
        PLT = sb("PLT", [128, 2, 256], BF16)
        W384 = [sb("W384_%d" % i, [128, 384], F32) for i in range(2)]
        ST6 = sb("ST6", [128, 2, 6], F32)
        SM = sb("SM", [128, 64], F32)
        PS = S.psum("PS", [128, 3584], F32)
        PST = S.psum("PST", [128, 1024], BF16)

        def bank(i):
            return ('ps', i)

        PSTK = [('ps', 7)]

        def psl(b, c0, n):
            return PS[:, b * 512 + c0: b * 512 + c0 + n]

        def act(out, in_, func, R, W, bias=None, scale=None):
            kw = {}
            if bias is not None:
                kw['bias'] = bias
            if scale is not None:
                kw['scale'] = scale
            S.op('act', lambda e: e.activation(out=out, in_=in_, func=func, **kw), reads=R, writes=W)

        def tt(eng, out, in0, in1, op, R, W):
            S.op(eng, lambda e: e.tensor_tensor(out=out, in0=in0, in1=in1, op=op), reads=R, writes=W)

        def ts(eng, out, in0, s1, s2, op0, op1, R, W):
            if s2 is None:
                S.op(eng, lambda e: e.tensor_scalar(out=out, in0=in0, scalar1=s1, scalar2=None, op0=op0), reads=R, writes=W)
            else:
                S.op(eng, lambda e: e.tensor_scalar(out=out, in0=in0, scalar1=s1, scalar2=s2, op0=op0, op1=op1), reads=R, writes=W)

        def stt(out, in0, scalar, in1, op0, op1, R, W):
            S.op('dve', lambda e: e.scalar_tensor_tensor(out=out, in0=in0, scalar=scalar, in1=in1, op0=op0, op1=op1), reads=R, writes=W)

        def cp(eng, out, in_, R, W):
            if eng == 'act':
                S.op('act', lambda e: e.copy(out=out, in_=in_), reads=R, writes=W)
            else:
                S.op(eng, lambda e: e.tensor_copy(out=out, in_=in_), reads=R, writes=W)

        def mm(out, lhsT, rhs, start, stop, R, W, inc=True):
            S.op('pe', lambda e: e.matmul(out, lhsT=lhsT, rhs=rhs, start=start, stop=stop), reads=R, writes=W, inc=inc)

        def tr(out, in_, R, W, inc=True):
            S.op('pe', lambda e: e.transpose(out=out, in_=in_, identity=IDN[:, :]), reads=list(R) + [IDN], writes=W, inc=inc)

        dq = ['sp', 'pool']
        dqi = [0]

        def dma(out, in_, R, W, q=None, **kw):
            if q is None:
                q = dq[dqi[0] % 2]
                dqi[0] += 1
            return S.dma(q, out, in_, reads=R, writes=W, **kw)

        def stop_at(name):
            if stop == name:
                raise _Stop()

        try:
            for k in C:
                dma(C[k][tuple(slice(None) for _ in CONST_SHAPES[k][0])], cd[k], [], [C[k]])
            S.op('dve', lambda e: e.memset(ZB[:, :, :], 0.0), writes=[ZB])
            dma(CTs[:, :, :], cT_d, [], [CTs])
            S.op('pool', lambda e: e.memset(R_sb[:, :], 0.0), writes=[R_sb])
            S.op('pool', lambda e: e.memset(U_sb[:, :], 0.0), writes=[U_sb])
            act(CTs[:, :, :], CTs[:, :, :], AF.Silu, [CTs], [CTs])

            for l in range(depth):
                last = (l == DEPTH - 1)
                stg = [HT, T1]
                si = 0
                for k in range(8):
                    for c0 in range(0, DIN, 1056):
                        sgt = stg[si % 2]
                        si += 1
                        dma(sgt[:, 0:1056], w_in[l, k * 128:(k + 1) * 128, c0:c0 + 1056], [], [sgt])
                        cp('pool', WIN[:, k, c0:c0 + 1056], sgt[:, 0:1056], [sgt], [WIN])
                for k in range(8):
                    sgt = stg[si % 2]
                    si += 1
                    dma(sgt[:, 0:D], w_out[l, k * 128:(k + 1) * 128, :], [], [sgt])
                    cp('pool', WOUT[:, k, :], sgt[:, 0:D], [sgt], [WOUT])
                dma(W384[0][:, :], w_up[l], [], [W384[0]])
                cp('pool', WUP[:, :], W384[0][:, :], [W384[0]], [WUP])
                dma(W384[1][:, :], a_up[l], [], [W384[1]])
                cp('pool', AUP[:, :], W384[1][:, :], [W384[1]], [AUP])
                dma(W384[0][:, 0:128].rearrange("p (a o) -> p a o", a=2), pw_d[l], [], [W384[0]])
                cp('pool', PW[:, :, :], W384[0][:, 0:128].rearrange("p (a o) -> p a o", a=2), [W384[0]], [PW])
                dma(PP[:, :], pp_d[l], [], [PP])
                dma(LNG[:, :], ln_g[l:l + 1, :].partition_broadcast(128), [], [LNG])
                dma(LNB[:, :], ln_b[l:l + 1, :].partition_broadcast(128), [], [LNB])
                dma(GNG[:, :], gn_g[l:l + 1, :].partition_broadcast(128), [], [GNG])
                dma(GNB[:, :], gn_b[l:l + 1, :].partition_broadcast(128), [], [GNB])
                dma(HGG[:, :], hg_g[l:l + 1, :].partition_broadcast(128), [], [HGG])
                dma(PSC[:, :], pscale[l:l + 1, :].partition_broadcast(128), [], [PSC])
                ts('dve', PD[:, 0:3], PP[:, 48:51], -1.0, 1.0, ALU.mult, ALU.add, [PP], [PD])
                if l == 0:
                    S.op('pool', lambda e: e.memset(PD[:, 3:9], 0.0), writes=[PD])
                    S.op('pool', lambda e: e.memset(PD[:, 9:15], 1.0), reads=[PD], writes=[PD])
                else:
                    for d in range(2):
                        tt('dve', PD[:, 3 + 3 * d:6 + 3 * d], PP[:, 60 + d * 6:63 + d * 6], PP[:, 57 + d * 6:60 + d * 6], ALU.subtract, [PP, PD], [PD])
                    act(PD[:, 3:9], PD[:, 3:9], AF.Sigmoid, [PD], [PD])
                    ts('dve', PD[:, 9:15], PD[:, 3:9], -1.0, 1.0, ALU.mult, ALU.add, [PD], [PD])
                stop_at('weights')
                for cg in range(24):
                    sgt = stg[si % 2]
                    si += 1
                    dma(sgt[:, 0:1024].rearrange("p (k n) -> p k n", k=8),
                        mod_w[l, :, cg * 128:(cg + 1) * 128].rearrange("(k p) n -> p k n", p=128), [], [sgt])
                    dma(MODB[:, :], mod_b[l:l + 1, cg * 128:(cg + 1) * 128].partition_broadcast(3), [], [MODB])
                    for k in range(8):
                        mm(PS[0:3, 1024:1024 + 128], CTs[:, k, :], sgt[:, k * 128:(k + 1) * 128], k == 0, k == 7, [CTs, sgt], [bank(2)], inc=(k == 7))
                    tt('dve', MODSB[:, :], PS[0:3, 1024:1024 + 128], MODB[:, :], ALU.add, [bank(2), MODB], [MODSB])
                    dma(MODS[:, cg * 128:(cg + 1) * 128], MODSB[:, :], [MODSB], ['MODS'], q='sp')
                for who in range(3):
                    dma(SCSH[:, who, :, :], MODS[who, 0:2048].rearrange("(j k p) -> p j k", j=2, k=8, p=128), ['MODS'], [SCSH],
                        q='sp', allow_slow_non_contiguous=True)
                ts('dve', SCSH[:, :, 1, :], SCSH[:, :, 1, :], 1.0, None, ALU.add, None, [SCSH], [SCSH])
                dbg('scsh', SCSH[:, :, :, :], [128, 3, 2, 8], F32, [SCSH])

                stop_at('mod')
                for b in range(nbatch):
                    h_in = hin[b] if l == 0 else H1[b]
                    for j in range(NTT):
                        who = 2 if j < 2 else b
                        base = 1 + j * 128 if j < 2 else 259 + (j - 2) * 128
                        dma(HT[:, 0:D], h_in[j * 128:(j + 1) * 128, :], ['H1'] if l > 0 else [], [HT])
                        for hh in range(2):
                            S.op('dve', lambda e, hh=hh: e.bn_stats(out=ST6[:, hh, :], in_=HT[:, hh * 512:(hh + 1) * 512]), reads=[HT, ST6], writes=[ST6])
                        S.op('dve', lambda e: e.bn_aggr(out=SM[:, 0:2], in_=ST6[:, :, :]), reads=[ST6], writes=[SM])
                        act(SM[:, 2:3], SM[:, 1:2], AF.Sqrt, [SM], [SM], bias=LN_EPS)
                        S.op('dve', lambda e: e.reciprocal(out=SM[:, 3:4], in_=SM[:, 2:3]), reads=[SM], writes=[SM])
                        ts('dve', SM[:, 4:5], SM[:, 0:1], SM[:, 3:4], -1.0, ALU.mult, ALU.mult, [SM], [SM])
                        act(XH[:, :], HT[:, 0:D], AF.Identity, [HT, SM], [XH], bias=SM[:, 4:5], scale=SM[:, 3:4])
                        for k in range(8):
                            tr(PST[:, k * 128:(k + 1) * 128], XH[:, k * 128:(k + 1) * 128], [XH], PSTK, inc=(k == 7))
                        xnt = XNT[j % 2]
                        for k in range(8):
                            if k % 2 == 0:
                                ts('dve', xnt[:, k, :], PST[:, k * 128:(k + 1) * 128], SCSH[:, who, 1, k:k + 1], SCSH[:, who, 0, k:k + 1],
                                   ALU.mult, ALU.add, PSTK + [SCSH], [xnt])
                            else:
                                act(xnt[:, k, :], PST[:, k * 128:(k + 1) * 128], AF.Identity, PSTK + [SCSH], [xnt],
                                    bias=SCSH[:, who, 0, k:k + 1], scale=SCSH[:, who, 1, k:k + 1])
                        dma(XND[:, :, base:base + 128], xnt[:, :, :], [xnt], ['XND'])

                    stop_at('xn')
                    blk_i = [0]
                    for rev in (True, False):
                        d = 1 if rev else 0
                        for hb in (HR32, HH32, HR16, HH16):
                            S.op('pool', lambda e, hb=hb: e.memset(hb[:, :], 0.0), reads=[hb], writes=[hb])
                        blocks = [(True, 0)] + [(False, t0) for t0 in (range(SEQ - NB, -1, -NB) if rev else range(0, SEQ, NB))]
                        for (is_ctx, t0) in blocks:
                            emit = not (is_ctx and last)
                            base = 1 if is_ctx else 259 + t0
                            jt0 = 0 if is_ctx else 2 + t0 // 128
                            first_dbg = (l == 0 and b == 0 and is_ctx)
                            tag = 'r' if rev else 'f'
                            if (not rev) and emit:
                                for t_ in range(NB // 128):
                                    dma(YBT[:, t_, :], YB[jt0 + t_], [('YB', jt0 + t_)], [YBT])
                                    dma(OBT[:, t_, :], OB[jt0 + t_], [('OB', jt0 + t_)], [OBT])
                                if is_ctx or t0 == 0:
                                    who_g = 2 if is_ctx else b
                                    dma(GT[:, :], MODS[who_g:who_g + 1, 2048:3072].partition_broadcast(128), ['MODS'], [GT], q='sp')

                            XN = XBS[blk_i[0] % 2]
                            blk_i[0] += 1
                            lz = is_ctx or t0 == 0
                            rz = is_ctx or t0 == SEQ - NB
                            dma(XN[:, :, (1 if lz else 0):(NB + 1 if rz else NB + 2)], XND[:, :, base - (0 if lz else 1): base + NB + (0 if rz else 1)], ['XND'], [XN])
                            if lz:
                                cp('dve', XN[:, :, 0:1], ZB[:, :, :], [ZB, XN], [XN])
                            if rz:
                                cp('dve', XN[:, :, NB + 1:NB + 2], ZB[:, :, :], [ZB, XN], [XN])
                            if first_dbg and rev:
                                dbg('xn', XN[:, :, :], [128, 8, NB + 2], BF16, [XN])

                            def xn_h(k):
                                return XN[:, k, 0: NB + 2]

                            def xn_c(k):
                                return XN[:, k, 1: NB + 1]

                            pb = [0]

                            def proj_fm(col0, halo):
                                bk = pb[0] % 2
                                pb[0] += 1
                                n = NB + 2 if halo else NB
                                for k in range(8):
                                    mm(psl(bk, 0, n), WIN[:, k, col0:col0 + 128], xn_h(k) if halo else xn_c(k), k == 0, k == 7, [WIN, XN], [bank(bk)], inc=(k == 7))
                                return bk

                            def shift3(c, dst, dst_buf, func=None):
                                bk = proj_fm(256 + c * 128, True)
                                t1, t2 = SCR[6], SCR[7]
                                act(t1[:, :], psl(bk, 1, NB), AF.Identity, [bank(bk), PP], [t1], scale=PP[:, 3 * c + 1:3 * c + 2])
                                stt(t2[:, :], psl(bk, 0, NB), PP[:, 3 * c:3 * c + 1], t1[:, :], ALU.mult, ALU.add, [bank(bk), PP, t1], [t2])
                                if func is None:
                                    stt(dst, psl(bk, 2, NB), PP[:, 3 * c + 2:3 * c + 3], t2[:, :], ALU.mult, ALU.add, [bank(bk), PP, t2], [dst_buf])
                                else:
                                    stt(t1[:, :], psl(bk, 2, NB), PP[:, 3 * c + 2:3 * c + 3], t2[:, :], ALU.mult, ALU.add, [bank(bk), PP, t2], [t1])
                                    act(dst, t1[:, :], func, [t1], [dst_buf])

                            for i in range(3):
                                shift3(i, R_t[i][:, :], R_t[i])
                                shift3(3 + i, K_t[i][:, :], K_t[i])
                                shift3(6 + i, V_b[i][:, :], V_b[i])
                            shift3(9, TANHA[:, :], TANHA, func=AF.Tanh)
                            shift3(10, XAB[:, :], XAB)
                            if first_dbg:
                                dbg('r0' + tag, R_t[0][:, :], [128, NB], F32, [R_t[0]])
                                dbg('v0' + tag, V_b[0][:, :], [128, NB], BF16, [V_b[0]])

                            stop_at('shift')
                            hp_d = slice(d * 64, d * 64 + 64)
                            for i in range(3):
                                sig, a_, kk, tmp, gi, ge = SCR[0], SCR[1], SCR[2], SCR[3], SCR[4], SCR[5]
                                e1, e2_, e3 = SCR[6], SCR[7], SCR[3]
                                mm(psl(2, 0, NB), WUP[hp_d, i * 128:(i + 1) * 128], TANHA[hp_d, :], True, True, [WUP, TANHA], [bank(2)])
                                act(sig[:, :], psl(2, 0, NB), AF.Sigmoid, [bank(2), PP], [sig], bias=PP[:, 33 + d * 3 + i:34 + d * 3 + i])
                                mm(psl(2, 0, NB), AUP[hp_d, i * 128:(i + 1) * 128], XAB[hp_d, :], True, True, [AUP, XAB], [bank(2)])
                                act(a_[:, :], psl(2, 0, NB), AF.Sigmoid, [bank(2), PP], [a_], bias=PP[:, 39 + d * 3 + i:40 + d * 3 + i])
                                act(SQB[:, :], K_t[i][:, :], AF.Square, [K_t[i], PP], [SQB], scale=PP[:, 45 + i:46 + i])
                                mm(psl(2, 0, NB), ONESBD[:, :], SQB[:, :], True, True, [ONESBD, SQB], [bank(2)])
                                act(tmp[:, :], psl(2, 0, NB), AF.Sqrt, [bank(2)], [tmp], bias=1e-12)
                                S.op('dve', lambda e, tmp=tmp: e.reciprocal(out=tmp[:, :], in_=tmp[:, :]), reads=[tmp], writes=[tmp])
                                stt(kk[:, :], K_t[i][:, :], PP[:, 45 + i:46 + i], tmp[:, :], ALU.mult, ALU.mult, [K_t[i], PP, tmp], [kk])
                                ts('dve', tmp[:, :], a_[:, :], PP[:, 48 + i:49 + i], PD[:, i:i + 1], ALU.mult, ALU.add, [a_, PP, PD], [tmp])
                                tt('pool', tmp[:, :], tmp[:, :], K_t[i][:, :], ALU.mult, [tmp, K_t[i]], [tmp])
                                stt(RKB[:, :], R_t[i][:, :], PP[:, 51 + d * 3 + i:52 + d * 3 + i], tmp[:, :], ALU.mult, ALU.mult, [R_t[i], PP, tmp], [RKB])
                                for t_ in range(NT):
                                    mm(PS[:, 3072 + 300 + t_ * 8 + 2 * i: 3072 + 300 + t_ * 8 + 2 * i + 2], RKB[:, t_ * 128:(t_ + 1) * 128], E2[:, :], True, True,
                                       [RKB, E2], [bank(6)], inc=(t_ == NT - 1))
                                stt(a_[:, :], a_[:, :], -1.0, kk[:, :], ALU.mult, ALU.mult, [a_, kk], [a_])
                                S.op('dve', lambda e, gi=gi, sig=sig: e.tensor_tensor_scan(out=gi[:, :], data0=MASK64[:, :], data1=sig[:, :], initial=0.0, op0=ALU.mult, op1=ALU.add),
                                     reads=[MASK64, sig], writes=[gi])
                                tt('pool', ge[:, :], gi[:, :], sig[:, :], ALU.subtract, [gi, sig], [ge])
                                act(GAM[:, i, :], gi[:, 63::64], AF.Exp, [gi], [GAM], scale=-CDEC)
                                if not rev:
                                    act(e1[:, :], gi[:, :], AF.Exp, [gi], [e1], scale=-CDEC)
                                    act(e2_[:, :], ge[:, :], AF.Exp, [ge], [e2_], scale=-CDEC)
                                    act(sig[:, :], gi[:, :], AF.Exp, [gi], [sig], scale=CDEC)
                                else:
                                    act(e1[:, :], ge[:, :], AF.Exp, [ge], [e1], scale=CDEC)
                                    act(e2_[:, :], gi[:, :], AF.Exp, [gi], [e2_], scale=CDEC)
                                    act(sig[:, :], ge[:, :], AF.Exp, [ge], [sig], scale=-CDEC)
                                tt('dve', RT[i][:, :], R_t[i][:, :], e1[:, :], ALU.mult, [R_t[i], e1], [RT[i]])
                                tt('pool', KP[i][:, :], kk[:, :], e2_[:, :], ALU.mult, [kk, e2_], [KP[i]])
                                tt('dve', KT[i][:, :], tmp[:, :], sig[:, :], ALU.mult, [tmp, sig], [KT[i]])
                                tt('pool', BN[i][:, :], a_[:, :], sig[:, :], ALU.mult, [a_, sig], [BN[i]])
                            sdst = S1 if rev else S0
                            if rev:
                                cp('act', S1[:, jt0:jt0 + NT, :], PS[:, 3072 + 300:3072 + 300 + NT * 8].rearrange("p (t c) -> p t c", c=8)[:, :, 0:6], [bank(6)], [S1])
                            else:
                                cp('act', S0[:, :, :], PS[:, 3072 + 300:3072 + 300 + NT * 8].rearrange("p (t c) -> p t c", c=8)[:, :, 0:6], [bank(6)], [S0])
                            if first_dbg:
                                dbg('rt0' + tag, RT[0][:, :], [128, NB], BF16, [RT[0]])
                                dbg('kp0' + tag, KP[0][:, :], [128, NB], BF16, [KP[0]])
                                dbg('kt0' + tag, KT[0][:, :], [128, NB], BF16, [KT[0]])
                                dbg('bn0' + tag, BN[0][:, :], [128, NB], BF16, [BN[0]])
                                dbg('gam' + tag, GAM[:, :, :], [128, 3, NB // 64], F32, [GAM])

                            stop_at('rprep')
                            NS = NB // 32
                            NC4 = NB // 64
                            v32 = lambda t_: t_[:, :].rearrange("p (c j) -> p c j", c=NS)
                            for i in range(3):
                                sg_, f_, lf, gi, ge, e1, e2_ = SCR[0], SCR[1], SCR[2], SCR[4], SCR[5], SCR[6], SCR[7]
                                bz = proj_fm(1664 + (2 + d) * 384 + i * 128, False)
                                act(sg_[:, :], psl(bz, 0, NB), AF.Sigmoid, [bank(bz)], [sg_])
                                ts('dve', f_[:, :], sg_[:, :], PD[:, 9 + 3 * d + i:10 + 3 * d + i], PD[:, 3 + 3 * d + i:4 + 3 * d + i], ALU.mult, ALU.add, [sg_, PD], [f_])
                                act(lf[:, :], f_[:, :], AF.Ln, [f_], [lf])
                                ts('pool', f_[:, :], f_[:, :], -1.0, 1.0, ALU.mult, ALU.add, [f_], [f_])
                                S.op('dve', lambda e, gi=gi, lf=lf: e.tensor_tensor_scan(out=gi[:, :], data0=C['mask32'][:, :], data1=lf[:, :], initial=0.0, op0=ALU.mult, op1=ALU.add),
                                     reads=[C['mask32'], lf], writes=[gi])
                                if not rev:
                                    gsrc = gi
                                else:
                                    tt('pool', ge[:, :], gi[:, :], lf[:, :], ALU.subtract, [gi, lf], [ge])
                                    gsrc = ge
                                tt('dve', v32(sg_), gi[:, 31::32].unsqueeze(2).to_broadcast([128, NS, 32]), v32(gsrc), ALU.subtract, [gi, gsrc], [sg_])
                                ts('dve', lf[:, :], gsrc[:, :], -80.0, None, ALU.max, None, [gsrc], [lf])
                                act(SM[:, 40:40 + NS], gi[:, 31::32], AF.Exp, [gi, SM], [SM])
                                tt('dve', GAMH[:, i, :], SM[:, 40:40 + NS:2], SM[:, 41:40 + NS:2], ALU.mult, [SM], [GAMH])
                                S.op('pool', lambda e: e.memset(FAC[:, :, :], 1.0), reads=[FAC], writes=[FAC])
                                if not rev:
                                    cp('dve', FAC[:, 0, 1::2], SM[:, 40:40 + NS:2], [SM, FAC], [FAC])
                                    cp('dve', FAC[:, 1, 0::2], SM[:, 41:40 + NS:2], [SM, FAC], [FAC])
                                    act(e1[:, :], gi[:, :], AF.Exp, [gi], [e1])
                                    act(e2_[:, :], lf[:, :], AF.Exp, [lf], [e2_], scale=-1.0)
                                else:
                                    cp('dve', FAC[:, 0, 0::2], SM[:, 41:40 + NS:2], [SM, FAC], [FAC])
                                    cp('dve', FAC[:, 1, 1::2], SM[:, 40:40 + NS:2], [SM, FAC], [FAC])
                                    act(e1[:, :], lf[:, :], AF.Exp, [lf], [e1], scale=-1.0)
                                    act(e2_[:, :], ge[:, :], AF.Exp, [ge], [e2_])
                                act(sg_[:, :], sg_[:, :], AF.Exp, [sg_], [sg_])
                                fq = FAC[:, 0, :].unsqueeze(2).to_broadcast([128, NS, 32])
                                fk = FAC[:, 1, :].unsqueeze(2).to_broadcast([128, NS, 32])
                                bq = proj_fm(1664 + i * 128, False)
                                tt('dve', QT[i][:, :], psl(bq, 0, NB), e1[:, :], ALU.mult, [bank(bq), e1], [QT[i]])
                                tt('pool', KH[i][:, :], f_[:, :], e2_[:, :], ALU.mult, [f_, e2_], [KH[i]])
                                dsc = M1p[:, 0:2, :].rearrange("p a t -> p (a t)")
                                tt('dve', dsc, psl(bq, 0, NB), f_[:, :], ALU.mult, [bank(bq), f_], [M1p])
                                for t_ in range(NT):
                                    mm(PS[:, 3072 + 340 + t_ * 8 + 2 * i: 3072 + 340 + t_ * 8 + 2 * i + 2], M1p[:, 0:2, :].rearrange("p a t -> p (a t)")[:, t_ * 128:(t_ + 1) * 128], E2[:, :], True, True,
                                       [M1p, E2], [bank(6)], inc=(t_ == NT - 1))
                                if not rev:
                                    tt('pool', QX[i][:, :], f_[:, :], sg_[:, :], ALU.mult, [f_, sg_], [QX[i]])
                                    tt('dve', v32(QI[i]), v32(QT[i]), fq, ALU.mult, [QT[i], FAC], [QI[i]])
                                    tt('pool', v32(KS[i]), v32(QX[i]), fk, ALU.mult, [QX[i], FAC], [KS[i]])
                                else:
                                    tt('dve', QX[i][:, :], psl(bq, 0, NB), sg_[:, :], ALU.mult, [bank(bq), sg_], [QX[i]])
                                    tt('dve', v32(QI[i]), v32(QX[i]), fq, ALU.mult, [QX[i], FAC], [QI[i]])
                                    tt('pool', v32(KS[i]), v32(KH[i]), fk, ALU.mult, [KH[i], FAC], [KS[i]])
                            if rev:
                                cp('act', SH1[:, jt0:jt0 + NT, :], PS[:, 3072 + 340:3072 + 340 + NT * 8].rearrange("p (t c) -> p t c", c=8)[:, :, 0:6], [bank(6)], [SH1])
                            else:
                                cp('act', SH0[:, :, :], PS[:, 3072 + 340:3072 + 340 + NT * 8].rearrange("p (t c) -> p t c", c=8)[:, :, 0:6], [bank(6)], [SH0])
                            if first_dbg:
                                dbg('qt0' + tag, QT[0][:, :], [128, NB], BF16, [QT[0]])
                                dbg('kh0' + tag, KH[0][:, :], [128, NB], BF16, [KH[0]])

                            stop_at('hprep')
                            for t_ in range(NT):
                                for k in range(8):
                                    mm(psl(2, 0, 384), XN[:, k, 1 + t_ * 128: 1 + (t_ + 1) * 128], WIN[:, k, 1664 + 384:1664 + 768], k == 0, k == 7, [XN, WIN], [bank(2)], inc=(k == 7))
                                cp('act', I_tm[:, t_, :], psl(2, 0, 384), [bank(2)], [I_tm])
                            for t_ in range(NT):
                                for gi_, (srcs, dst) in enumerate(((V_b, V_tm), (KT, K_tm), (BN, B_tm), (KS, KH_tm))):
                                    for i in range(3):
                                        tr(PST[:, (gi_ % 2) * 512 + i * 128:(gi_ % 2) * 512 + (i + 1) * 128], srcs[i][:, t_ * 128:(t_ + 1) * 128], [srcs[i]], PSTK, inc=(i == 2))
                                    cp('dve' if gi_ % 2 == 0 else 'act', dst[:, t_, :], PST[:, (gi_ % 2) * 512:(gi_ % 2) * 512 + 384], PSTK, [dst])
                            if first_dbg:
                                dbg('vtm' + tag, V_tm[:, :, :], [128, NT, 384], BF16, [V_tm])
                                dbg('itm' + tag, I_tm[:, :, :], [128, NT, 384], BF16, [I_tm])

                            stop_at('trans')
                            tts = list(range(NT))
                            if rev:
                                tts = tts[::-1]
                            mk_ = lambda m_: MASKS[:, m_, :].unsqueeze(1).unsqueeze(1).to_broadcast([128, 2, 3, 128])
                            idn_bc = IDN[:, :].unsqueeze(1).unsqueeze(1).to_broadcast([128, 2, 3, 128])
                            if not rev:
                                mA, mP, mX0, mT0, mT0T = mk_(0), mk_(1), mk_(2), mk_(4), mk_(5)
                            else:
                                mA, mP, mX0, mT0, mT0T = mk_(2), mk_(3), mk_(0), mk_(5), mk_(4)
                            BIGA, BIGB, BIGC = (3, 4), (5, 6), (0, 1)

                            def bigh(reg, h, w=128):
                                c0 = reg[h % 2] * 512 + (h // 2) * w
                                return PS[:, c0:c0 + w]

                            def big(reg, w=128, rows=slice(0, 128)):
                                v = PS[rows, reg[0] * 512:(reg[0] + 2) * 512].rearrange("p (b c) -> p b c", b=2)[:, :, 0:3 * w]
                                return v.rearrange("p b (j w) -> p b j w", j=3)

                            def sbv(buf_ap):
                                return buf_ap.rearrange("p (j b) w -> p b j w", b=2)

                            def bigk(reg):
                                return [bank(reg[0]), bank(reg[1])]

                            def hsl(bufs, h, t_):
                                return bufs[h // 2][(h % 2) * 64:(h % 2) * 64 + 64, t_ * 128:(t_ + 1) * 128]

                            for t_ in tts:
                                tsl = slice(t_ * 128, (t_ + 1) * 128)
                                for h in range(6):
                                    mm(bigh(BIGA, h), hsl(KT, h, t_), hsl(KP, h, t_), True, True, [KT[h // 2], KP[h // 2]], bigk(BIGA), inc=(h == 5))
                                tt('dve', sbv(AT_sb[:, :, :]), big(BIGA), mA, ALU.mult, bigk(BIGA) + [MASKS], [AT_sb])
                                stop_at('inv1')
                                if emit:
                                    for h in range(6):
                                        mm(bigh(BIGB, h), hsl(KT, h, t_), hsl(RT, h, t_), True, True, [KT[h // 2], RT[h // 2]], bigk(BIGB), inc=(h == 5))
                                    tt('dve', sbv(PT_sb[:, :, :]), big(BIGB), mP, ALU.mult, bigk(BIGB) + [MASKS], [PT_sb])
                                for h in range(6):
                                    mm(bigh(BIGA, h), hsl(BN, h, t_), hsl(KP, h, t_), True, True, [BN[h // 2], KP[h // 2]], bigk(BIGA), inc=(h == 5))
                                tt('dve', sbv(XTb[:, :, :]), big(BIGA), mA, ALU.mult, bigk(BIGA) + [MASKS], [XTb])
                                tt('dve', sbv(TTb[:, :, :]), big(BIGA), mT0T, ALU.mult, bigk(BIGA) + [MASKS], [TTb])
                                tt('dve', sbv(TTb[:, :, :]), sbv(TTb[:, :, :]), idn_bc, ALU.add, [TTb, IDN], [TTb])
                                if emit:
                                    for h in range(6):
                                        mm(bigh(BIGB, h), hsl(BN, h, t_), hsl(RT, h, t_), True, True, [BN[h // 2], RT[h // 2]], bigk(BIGB), inc=(h == 5))
                                    tt('dve', sbv(QN_sb[:, :, :]), big(BIGB), mP, ALU.mult, bigk(BIGB) + [MASKS], [QN_sb])
                                for h in range(6):
                                    mm(bigh(BIGB, h), hsl(KP, h, t_), hsl(BN, h, t_), True, True, [KP[h // 2], BN[h // 2]], bigk(BIGB), inc=(h == 5))
                                tt('dve', sbv(XA[:, :, :]), big(BIGB), mX0, ALU.mult, bigk(BIGB) + [MASKS], [XA])
                                tt('dve', sbv(TNb[:, :, :]), big(BIGB), mT0, ALU.mult, bigk(BIGB) + [MASKS], [TNb])
                                tt('dve', sbv(TNb[:, :, :]), sbv(TNb[:, :, :]), idn_bc, ALU.add, [TNb, IDN], [TNb])
                                if emit:
                                    for h in range(6):
                                        mm(bigh(BIGA, h), hsl(KH, h, t_), hsl(QT, h, t_), True, True, [KH[h // 2], QT[h // 2]], bigk(BIGA), inc=(h == 5))
                                    for h in range(6):
                                        if not rev:
                                            mm(bigh(BIGC, h), hsl(QX, h, t_), hsl(QT, h, t_), True, True, [QX[h // 2], QT[h // 2]], bigk(BIGC), inc=(h == 5))
                                        else:
                                            mm(bigh(BIGC, h), hsl(KH, h, t_), hsl(QX, h, t_), True, True, [KH[h // 2], QX[h // 2]], bigk(BIGC), inc=(h == 5))
                                    tt('dve', sbv(PH_sb[:, :, :]), big(BIGA), mk_(16 if not rev else 17), ALU.mult, bigk(BIGA) + [MASKS], [PH_sb])
                                    tt('dve', sbv(M1b[:, :, :]), big(BIGC), mk_(18 if not rev else 19), ALU.mult, bigk(BIGC) + [MASKS], [M1b])
                                    tt('pool', PH_sb[:, :, :], PH_sb[:, :, :], M1b[:, :, :], ALU.add, [PH_sb, M1b], [PH_sb])
                                stop_at('inv2')
                                for lev in range(1, 6):
                                    mk = mk_((6 if not rev else 11) + lev - 1)
                                    mkT = mk_((11 if not rev else 6) + lev - 1)
                                    last_lev = (lev == 5)
                                    if not last_lev:
                                        for h in range(6):
                                            mm(bigh(BIGA, h), XTb[:, h, :], TNb[:, h, :], True, True, [XTb, TNb], bigk(BIGA), inc=(h == 5))
                                    for h in range(6):
                                        mm(bigh(BIGB, h), XA[:, h, :], TTb[:, h, :], True, True, [XA, TTb], bigk(BIGB), inc=(h == 5))
                                    if not last_lev:
                                        tt('dve', sbv(M1b[:, :, :]), big(BIGA), mk, ALU.mult, bigk(BIGA) + [MASKS], [M1b])
                                    tt('dve', sbv(M1p[:, :, :]), big(BIGB), mkT, ALU.mult, bigk(BIGB) + [MASKS], [M1p])
                                    if not last_lev:
                                        for h in range(6):
                                            mm(bigh(BIGA, h), TTb[:, h, :], M1b[:, h, :], True, True, [TTb, M1b], bigk(BIGA), inc=(h == 5))
                                    for h in range(6):
                                        mm(bigh(BIGC, h), TNb[:, h, :], M1p[:, h, :], True, True, [TNb, M1p], bigk(BIGC), inc=(h == 5))
                                    if not last_lev:
                                        tt('dve', sbv(TNb[:, :, :]), big(BIGA), sbv(TNb[:, :, :]), ALU.add, bigk(BIGA) + [TNb], [TNb])
                                    tt('dve', sbv(TTb[:, :, :]), big(BIGC), sbv(TTb[:, :, :]), ALU.add, bigk(BIGC) + [TTb], [TTb])
                                TTf = TTb
                                if first_dbg and t_ == 0:
                                    dbg('at' + tag, AT_sb[:, :, :], [128, 6, 128], BF16, [AT_sb])
                                    dbg('ttf' + tag, TTf[:, :, :], [128, 6, 128], BF16, [TTf])

                                stop_at('inv')
                                halves = (1, 0) if rev else (0, 1)
                                for half in halves:
                                    c_ = t_ * 2 + half
                                    hp = slice(half * 64, half * 64 + 64)
                                    gam_bc = GAM[:, :, c_:c_ + 1].to_broadcast([128, 3, 128])
                                    gamh_bc = GAMH[:, :, c_:c_ + 1].to_broadcast([128, 3, 128])
                                    H3 = lambda hb: hb[:, :].rearrange("p (a v) -> p a v", a=3)
                                    if rev:
                                        tt('dve', H3(HR32), H3(HR32), gam_bc, ALU.mult, [HR32, GAM], [HR32])
                                        cp('act', HR16[:, :], HR32[:, :], [HR32], [HR16])

                                    def hop(hb16, h):
                                        pr = h // 2
                                        o = (h % 2) * 64
                                        return hb16[o:o + 64, pr * 128 + o: pr * 128 + o + 64]
                                    RY, YH = (0, 1), (3, 4)
                                    h64 = lambda ap: ap.rearrange("p (h v) -> p h v", h=6)
                                    for h in range(6):
                                        mm(bigh(RY, h, 64), hsl(KP, h, t_), hop(HR16, h), True, False, [KP[h // 2], HR16], bigk(RY), inc=False)
                                        mm(bigh(RY, h, 64), AT_sb[:, h, :], V_tm[:, t_, h * 64:(h + 1) * 64], False, True, [AT_sb, V_tm], bigk(RY), inc=(h == 5))
                                    cp('dve', sbv(h64(R_sb[hp, :])), big(RY, 64, hp), bigk(RY), [R_sb])
                                    if emit:
                                        for h in range(6):
                                            mm(bigh(YH, h, 64), hsl(QI, h, t_), hop(HH16, h), True, False, [QI[h // 2], HH16], bigk(YH), inc=False)
                                            mm(bigh(YH, h, 64), PH_sb[:, h, :], I_tm[:, t_, h * 64:(h + 1) * 64], False, True, [PH_sb, I_tm], bigk(YH), inc=(h == 5))
                                    for pr in range(3):
                                        mm(psl(6, pr * 128, 128), KH_tm[hp, t_, pr * 128:(pr + 1) * 128], I_tm[hp, t_, pr * 128:(pr + 1) * 128], True, True, [KH_tm, I_tm], [bank(6)], inc=(pr == 2))
                                    for h in range(6):
                                        mm(psl(2, h * 64, 64), TTf[hp, h, :], R_sb[hp, h * 64:(h + 1) * 64], True, True, [TTf, R_sb], [bank(2)], inc=(h == 5))
                                    cp('act', U_sb[hp, :], PS[hp, 1024:1024 + 384], [bank(2)], [U_sb])
                                    if emit:
                                        for h in range(6):
                                            mm(bigh(RY, h, 64), hsl(RT, h, t_), hop(HR16, h), True, False, [RT[h // 2], HR16], bigk(RY), inc=False)
                                            mm(bigh(RY, h, 64), PT_sb[:, h, :], V_tm[:, t_, h * 64:(h + 1) * 64], False, False, [PT_sb, V_tm], bigk(RY), inc=False)
                                            mm(bigh(RY, h, 64), QN_sb[:, h, :], U_sb[:, h * 64:(h + 1) * 64], False, True, [QN_sb, U_sb], bigk(RY), inc=(h == 5))
                                    for pr in range(3):
                                        mm(psl(5, pr * 128, 128), K_tm[hp, t_, pr * 128:(pr + 1) * 128], V_tm[hp, t_, pr * 128:(pr + 1) * 128], True, False, [K_tm, V_tm], [bank(5)], inc=False)
                                        mm(psl(5, pr * 128, 128), B_tm[hp, t_, pr * 128:(pr + 1) * 128], U_sb[hp, pr * 128:(pr + 1) * 128], False, True, [B_tm, U_sb], [bank(5)], inc=(pr == 2))
                                    if emit:
                                        if rev:
                                            cp('act', sbv(h64(YST[hp, t_, :])), big(RY, 64, hp), bigk(RY), [YST])
                                            cp('act', sbv(h64(OST[hp, t_, :])), big(YH, 64, hp), bigk(YH), [OST])
                                        else:
                                            tt('dve', sbv(h64(Y_sb[hp, t_, :])), big(RY, 64, hp), sbv(h64(YBT[hp, t_, :])), ALU.add, bigk(RY) + [YBT], [Y_sb])
                                            tt('dve', sbv(h64(O_sb[hp, t_, :])), big(YH, 64, hp), sbv(h64(OBT[hp, t_, :])), ALU.add, bigk(YH) + [OBT], [O_sb])
                                    tt('dve', HR32[:, :], HR32[:, :], psl(5, 0, 384), ALU.add, [HR32, bank(5)], [HR32])
                                    tt('pool', H3(HH32), H3(HH32), gamh_bc, ALU.mult, [HH32, GAMH], [HH32])
                                    tt('dve', HH32[:, :], HH32[:, :], psl(6, 0, 384), ALU.add, [HH32, bank(6)], [HH32])
                                    cp('pool', HH16[:, :], HH32[:, :], [HH32], [HH16])
                                    if not rev:
                                        tt('pool', H3(HR32), H3(HR32), gam_bc, ALU.mult, [HR32, GAM], [HR32])
                                        cp('act', HR16[:, :], HR32[:, :], [HR32], [HR16])
                                if first_dbg and t_ == (0 if rev else NT - 1):
                                    dbg('u' + tag, U_sb[:, :], [128, 384], BF16, [U_sb])
                                    dbg('hr' + tag, HR32[:, :], [128, 384], F32, [HR32])
                                    dbg('hh' + tag, HH32[:, :], [128, 384], F32, [HH32])

                            if rev:
                                if emit:
                                    for t_ in range(NT):
                                        dma(YB[jt0 + t_], YST[:, t_, :], [YST], [('YB', jt0 + t_)])
                                        dma(OB[jt0 + t_], OST[:, t_, :], [OST], [('OB', jt0 + t_)])
                                    if first_dbg:
                                        dbg('yst', YST[:, :, :], [128, NT, 384], BF16, [YST])
                                        dbg('ost', OST[:, :, :], [128, NT, 384], BF16, [OST])
                                stop_at('chunk')
                                continue
                            if not emit:
                                continue
                            if first_dbg:
                                dbg('ysb', Y_sb[:, :, :], [128, NT, 384], BF16, [Y_sb])
                                dbg('osb', O_sb[:, :, :], [128, NT, 384], BF16, [O_sb])

                            for t_ in range(NT):
                                xcol = 1 + t_ * 128
                                for k in range(8):
                                    mm(psl(2, 0, 256), XN[:, k, xcol:xcol + 128], WIN[:, k, 0:256], k == 0, k == 7, [XN, WIN], [bank(2)], inc=(k == 7))
                                cp('act', PV_tm[:, t_, :], psl(2, 0, 256), [bank(2)], [PV_tm])
                            for t_ in range(NT):
                                jg = jt0 + t_
                                xcol = 1 + t_ * 128
                                for hh in range(2):
                                    for k in range(8):
                                        mm(psl(3 + hh, 0, 512), XN[:, k, xcol:xcol + 128], WIN[:, k, 3200 + hh * 512:3200 + (hh + 1) * 512], k == 0, k == 7, [XN, WIN], [bank(3 + hh)], inc=(k == 7))
                                    act(SG[:, hh * 512:(hh + 1) * 512], psl(3 + hh, 0, 512), AF.Silu, [bank(3 + hh)], [SG])
                                for ct in range(2):
                                    if is_ctx:
                                        for st_ in range(NT):
                                            mm(psl(2, 0, 256), PV_tm[:, st_, ct * 128:(ct + 1) * 128],
                                               C['pool_ctx'][:, st_, 2 * ct:2 * ct + 2, t_ * 128:(t_ + 1) * 128], st_ == 0, st_ == NT - 1, [PV_tm, C['pool_ctx']], [bank(2)], inc=(st_ == NT - 1))
                                    else:
                                        mm(psl(2, 0, 256), PV_tm[:, t_, ct * 128:(ct + 1) * 128], C['pool_lat'][:, 2 * ct:2 * ct + 2, :], True, True, [PV_tm, C['pool_lat']], [bank(2)])
                                    cp('dve', PLT[:, ct, 0:256], psl(2, 0, 256), [bank(2)], [PLT])
                                for g_ in range(4):
                                    ct, gh = g_ // 2, g_ % 2
                                    mm(psl(5 + gh, ct * 64, 64), PLT[gh * 64:gh * 64 + 64, ct, gh * 128:gh * 128 + 128], PW[gh * 64:gh * 64 + 64, ct, :], True, True, [PLT, PW], [bank(5), bank(6)], inc=(g_ == 3))
                                w0_, w1_ = W384
                                w2_ = w0_
                                tt('dve', w0_[:, 0:256].rearrange("p (ct gh o) -> p gh ct o", ct=2, gh=2), PS[:, 5 * 512:7 * 512].rearrange("p (gh c) -> p gh c", gh=2)[:, :, 0:128].rearrange("p gh (ct o) -> p gh ct o", ct=2),
                                   PSC[:, :].rearrange("p (ct gh o) -> p gh ct o", ct=2, gh=2), ALU.mult, [bank(5), bank(6), PSC], [w0_])
                                tt('pool', MIX[:, 0:256], w0_[:, 0:256], SG[:, 0:256], ALU.mult, [w0_, SG], [MIX])
                                y3 = Y_sb[:, t_, :].rearrange("p (h v) -> p h v", h=6)
                                S.op('dve', lambda e, y3=y3: e.tensor_reduce(out=SM[:, 8:14], in_=y3, axis=AX.X, op=ALU.add), reads=[Y_sb, SM], writes=[SM])
                                act(w0_[:, :], Y_sb[:, t_, :], AF.Square, [Y_sb], [w0_])
                                S.op('dve', lambda e, w0_=w0_: e.tensor_reduce(out=SM[:, 14:20], in_=w0_[:, :].rearrange("p (h v) -> p h v", h=6), axis=AX.X, op=ALU.add), reads=[w0_, SM], writes=[SM])
                                ts('dve', SM[:, 8:14], SM[:, 8:14], 1.0 / 64, None, ALU.mult, None, [SM], [SM])
                                tt('dve', SM[:, 20:26], SM[:, 8:14], SM[:, 8:14], ALU.mult, [SM], [SM])
                                stt(SM[:, 14:20], SM[:, 14:20], 1.0 / 64, SM[:, 20:26], ALU.mult, ALU.subtract, [SM], [SM])
                                act(SM[:, 14:20], SM[:, 14:20], AF.Sqrt, [SM], [SM], bias=GN_EPS)
                                S.op('dve', lambda e: e.reciprocal(out=SM[:, 14:20], in_=SM[:, 14:20]), reads=[SM], writes=[SM])
                                w13 = w1_[:, :].rearrange("p (h v) -> p h v", h=6)
                                tt('dve', w13, y3, SM[:, 8:14].unsqueeze(2).to_broadcast([128, 6, 64]), ALU.subtract, [Y_sb, SM], [w1_])
                                tt('dve', w13, w13, SM[:, 14:20].unsqueeze(2).to_broadcast([128, 6, 64]), ALU.mult, [w1_, SM], [w1_])
                                tt('pool', w1_[:, :], w1_[:, :], GNG[:, :], ALU.mult, [w1_, GNG], [w1_])
                                tt('pool', w1_[:, :], w1_[:, :], GNB[:, :], ALU.add, [w1_, GNB], [w1_])
                                tt('dve', SM[:, 26:32], S0[:, t_, :], S1[:, jg, :], ALU.add, [S0, S1, SM], [SM])
                                tt('dve', w0_[:, :].rearrange("p (h v) -> p h v", h=6), V_tm[:, t_, :].rearrange("p (h v) -> p h v", h=6),
                                   SM[:, 26:32].unsqueeze(2).to_broadcast([128, 6, 64]), ALU.mult, [V_tm, SM], [w0_])
                                tt('pool', w1_[:, :], w1_[:, :], w0_[:, :], ALU.add, [w1_, w0_], [w1_])
                                tt('dve', MIX[:, 256:640], w1_[:, :], SG[:, 256:640], ALU.mult, [w1_, SG], [MIX])
                                tt('dve', SM[:, 32:38], SH0[:, t_, :], SH1[:, jg, :], ALU.add, [SH0, SH1, SM], [SM])
                                o3 = w1_[:, :].rearrange("p (h v) -> p h v", h=6)
                                tt('dve', o3, I_tm[:, t_, :].rearrange("p (h v) -> p h v", h=6), SM[:, 32:38].unsqueeze(2).to_broadcast([128, 6, 64]), ALU.mult, [I_tm, SM], [w1_])
                                tt('pool', w1_[:, :], w1_[:, :], O_sb[:, t_, :], ALU.add, [w1_, O_sb], [w1_])
                                act(w2_[:, :], w1_[:, :], AF.Square, [w1_], [w2_])
                                S.op('dve', lambda e, w2_=w2_: e.tensor_reduce(out=SM[:, 20:26], in_=w2_[:, :].rearrange("p (h v) -> p h v", h=6), axis=AX.X, op=ALU.add), reads=[w2_, SM], writes=[SM])
                                act(SM[:, 20:26], SM[:, 20:26], AF.Sqrt, [SM], [SM], bias=RMS_EPS, scale=1.0 / 64)
                                S.op('dve', lambda e: e.reciprocal(out=SM[:, 20:26], in_=SM[:, 20:26]), reads=[SM], writes=[SM])
                                tt('dve', w2_[:, :].rearrange("p (h v) -> p h v", h=6), o3, SM[:, 20:26].unsqueeze(2).to_broadcast([128, 6, 64]), ALU.mult, [w1_, SM], [w2_])
                                tt('pool', w2_[:, :], w2_[:, :], HGG[:, :], ALU.mult, [w2_, HGG], [w2_])
                                tt('dve', MIX[:, 640:1024], w2_[:, :], SG[:, 640:1024], ALU.mult, [w2_, SG], [MIX])
                                if first_dbg and t_ == 0:
                                    dbg('mix', MIX[:, :], [128, D], BF16, [MIX])
                                for k in range(8):
                                    tr(PST[:, k * 128:(k + 1) * 128], MIX[:, k * 128:(k + 1) * 128], [MIX], PSTK, inc=(k == 7))
                                cp('act', MIXT[:, :, :], PST[:, :].rearrange("p (k t) -> p k t", k=8), PSTK, [MIXT])
                                for hh in range(2):
                                    for k in range(8):
                                        mm(psl(3 + hh, 0, 512), MIXT[:, k, :], WOUT[:, k, hh * 512:(hh + 1) * 512], k == 0, k == 7, [MIXT, WOUT], [bank(3 + hh)], inc=(k == 7))
                                dma(HT[:, 0:D], h_in[jg * 128:(jg + 1) * 128, :], ['H1'] if l > 0 else [], [HT])
                                gt_t = GT
                                tt('dve', T1[:, 0:D], PS[:, 3 * 512:3 * 512 + D], gt_t[:, :], ALU.mult, [bank(3), bank(4), gt_t], [T1])
                                stt(T1[:, 0:D], HT[:, 0:D], ALPHA, T1[:, 0:D], ALU.mult, ALU.add, [HT, T1], [T1])
                                for hh in range(2):
                                    S.op('dve', lambda e, hh=hh: e.bn_stats(out=ST6[:, hh, :], in_=T1[:, hh * 512:(hh + 1) * 512]), reads=[T1, ST6], writes=[ST6])
                                S.op('dve', lambda e: e.bn_aggr(out=SM[:, 0:2], in_=ST6[:, :, :]), reads=[ST6], writes=[SM])
                                act(SM[:, 2:3], SM[:, 1:2], AF.Sqrt, [SM], [SM], bias=LN_EPS)
                                S.op('dve', lambda e: e.reciprocal(out=SM[:, 3:4], in_=SM[:, 2:3]), reads=[SM], writes=[SM])
                                ts('dve', SM[:, 4:5], SM[:, 0:1], SM[:, 3:4], -1.0, ALU.mult, ALU.mult, [SM], [SM])
                                act(T1[:, 0:D], T1[:, 0:D], AF.Identity, [T1, SM], [T1], bias=SM[:, 4:5], scale=SM[:, 3:4])
                                tt('pool', T1[:, 0:D], T1[:, 0:D], LNG[:, :], ALU.mult, [T1, LNG], [T1])
                                tt('dve', T1[:, 0:D], T1[:, 0:D], LNB[:, :], ALU.add, [T1, LNB], [T1])
                                if last:
                                    r0 = (jg - 2) * 128
                                    S.out_tokens.append(dma(out_d[b, r0:r0 + 128, :], T1[:, 0:D], [T1], []))
                                else:
                                    dma(H1[b, jg * 128:(jg + 1) * 128, :], T1[:, 0:D], [T1], ['H1'])
                                    if first_dbg and t_ == 0:
                                        dbg('hout', T1[:, 0:D], [128, D], F32, [T1])
                            stop_at('out')
        except _Stop:
            pass
        S.finish()
        S.emit()
        n_instr = S.n_instr
    return nc, dbg_outs, n_instr


_CACHE = {}


def host_layout(inputs, core):
    f32 = np.float32
    g = lambda k: np.asarray(inputs[k], dtype=f32)
    b0 = core * NB_CORE
    m = {}
    m['hin'] = np.ascontiguousarray(np.concatenate([g('ctx')[b0:b0 + NB_CORE], g('x')[b0:b0 + NB_CORE]], axis=1))
    cv = np.stack([g('c')[b0], g('c')[b0 + 1], g('c_ctx')])
    m['cT'] = np.ascontiguousarray(cv.reshape(3, 8, 128).transpose(2, 1, 0))
    return m


def host_shared(inputs):
    f32 = np.float32
    g = lambda k: np.asarray(inputs[k], dtype=f32)
    m = {}
    for k in ('mod_w', 'mod_b', 'w_in', 'w_out', 'ln_g', 'ln_b'):
        m[k] = np.ascontiguousarray(g(k))
    m['w_up'] = np.ascontiguousarray(g('rwkv_w_up').reshape(DEPTH, 128, 384))
    m['a_up'] = np.ascontiguousarray(g('rwkv_a_up').reshape(DEPTH, 128, 384))
    pw = g('pool_w').reshape(DEPTH, 2, 2, 64, 64)
    m['pw'] = np.ascontiguousarray(pw.transpose(0, 2, 3, 1, 4).reshape(DEPTH, 128, 2, 64))
    pp = np.zeros((DEPTH, 128, NPP), f32)
    sh = g('rwkv_shift')
    for l in range(DEPTH):
        pp[l, :, 0:33] = sh[l].reshape(3, 11, 128).transpose(2, 1, 0).reshape(128, 33)
        pp[l, :, 33:39] = g('rwkv_w0')[l].reshape(2, 3, 128).transpose(2, 0, 1).reshape(128, 6)
        pp[l, :, 39:45] = g('rwkv_a0')[l].reshape(2, 3, 128).transpose(2, 0, 1).reshape(128, 6)
        pp[l, :, 45:48] = g('rwkv_k_k')[l].reshape(3, 128).T
        pp[l, :, 48:51] = g('rwkv_k_a')[l].reshape(3, 128).T
        pp[l, :, 51:57] = g('rwkv_r_k')[l].reshape(2, 3, 128).transpose(2, 0, 1).reshape(128, 6)
        lbl = g('hgrn_lb_logits')
        pp[l, :, 57:69] = lbl.reshape(2, DEPTH, 3, 128).transpose(3, 0, 1, 2).reshape(128, 12)
    m['pp'] = pp
    m['gn_g'] = np.ascontiguousarray(g('rwkv_gn_g'))
    m['gn_b'] = np.ascontiguousarray(g('rwkv_gn_b'))
    m['hg_g'] = np.ascontiguousarray(g('hgrn_norm_g'))
    m['pscale'] = np.ascontiguousarray(g('pool_scale'))
    for k, v in make_consts().items():
        m['c_' + k] = v
    return m


def kernel(**inputs):
    if 'nc' not in _CACHE:
        _CACHE['nc'] = build_program()[0]
    nc = _CACHE['nc']
    shared = host_shared(inputs)
    in_maps = []
    for core in range(8):
        m = dict(shared)
        m.update(host_layout(inputs, core))
        in_maps.append(m)
    res = run_bass_kernel_spmd(nc, in_maps, core_ids=list(range(8)))
    out = np.concatenate([np.asarray(r["out"], dtype=np.float32) for r in res.results], axis=0)
    return out
```
